# Optimizing a Trainium2 kernel written in Bass

```python
import math
import jax, jax.numpy as jnp
from jax import lax
import numpy as np

D_MODEL = 1024
BATCH = 4
SEQ = 4096
DEPTH = 1
DEC_BATCH = 32
DEC_SEQ = 1
PAST_LEN = 16384
PAGE_SIZE = 128

RET_DK = 256
RET_HEADS = D_MODEL // RET_DK
RET_DV = 2 * RET_DK
RET_QK = RET_HEADS * RET_DK
RET_V = RET_HEADS * RET_DV
RET_CHUNK = 128
ROPE_BASE = 10000.0
ATT_GROUPS = ((128, 1), (512, 4), (2048, 16))
N_GROUPS = len(ATT_GROUPS)
ATT_HPG = 4
ATT_HD = 128
ATT_HEADS = N_GROUPS * ATT_HPG
ATT_W = ATT_HEADS * ATT_HD
ATT_OUT = ATT_HPG * ATT_HD
ATT_STREAM_WIN = 128
FFN_HIDDEN = -(-8 * D_MODEL // (3 * 256)) * 256
EPS = 1e-6
IN_WIDTHS = (RET_QK, RET_QK, RET_V, RET_V, ATT_W, ATT_W, ATT_W, D_MODEL, D_MODEL)
IN_TOTAL = sum(IN_WIDTHS)
IN_SPLIT_IDX = tuple(int(c) for c in np.cumsum(IN_WIDTHS)[:-1])

kernel_name = "retnet_longnet_gated_hybrid"


def rmsnorm(x, g):
    xf = x.astype(jnp.float32)
    y = xf * lax.rsqrt(jnp.mean(xf * xf, axis=-1, keepdims=True) + EPS)
    return (y * g.astype(jnp.float32)).astype(x.dtype)


def rotary(x, pos):
    half = x.shape[-1] // 2
    inv = ROPE_BASE ** (-jnp.arange(half, dtype=jnp.float32) / half)
    ang = pos.astype(jnp.float32)[:, None] * inv[None, :]
    cos = jnp.cos(ang)[None, :, None, :]
    sin = jnp.sin(ang)[None, :, None, :]
    x1, x2 = x[..., :half], x[..., half:]
    return jnp.concatenate([x1 * cos - x2 * sin, x1 * sin + x2 * cos], axis=-1)


def retention(q, k, v, s0, chunk):
    B, T, H, dk = q.shape
    dv = v.shape[-1]
    n = T // chunk
    lg = jnp.log(1.0 - 2.0 ** (-5.0 - jnp.arange(H, dtype=jnp.float32)))
    i = jnp.arange(chunk, dtype=jnp.float32)
    diff = i[:, None] - i[None, :]
    d_intra = jnp.where(diff >= 0, jnp.exp(lg[:, None, None] * jnp.maximum(diff, 0.0)), 0.0)
    q_dec = jnp.exp(lg[:, None] * (i + 1.0))[:, :, None]
    k_dec = jnp.exp(lg[:, None] * (chunk - 1.0 - i))[:, :, None]
    c_dec = jnp.exp(lg * chunk)[:, None, None]

    def to_chunks(a):
        return a.reshape(B, n, chunk, H, a.shape[-1]).transpose(1, 0, 3, 2, 4)

    def step(S, xs):
        qc, kc, vc = xs
        sc = jnp.einsum("bhid,bhjd->bhij", qc, kc) * d_intra
        o = jnp.einsum("bhij,bhjv->bhiv", sc, vc) + jnp.einsum("bhid,bhdv->bhiv", qc * q_dec, S)
        S_new = S * c_dec + jnp.einsum("bhjd,bhjv->bhdv", kc * k_dec, vc)
        return S_new, o

    s_fin, o = lax.scan(step, s0, (to_chunks(q), to_chunks(k), to_chunks(v)))
    o = o.transpose(1, 0, 3, 2, 4).reshape(B, T, H, dv)
    return o, s_fin


def dilated_prompt(q, k, v, dil):
    B, T, H, E = q.shape
    M = ATT_STREAM_WIN
    L = T // dil
    nb = -(-L // M)
    Lp = nb * M

    def stream(a):
        a = a.reshape(B, L, dil, H, E).transpose(0, 2, 1, 3, 4)
        return jnp.pad(a, ((0, 0), (0, 0), (0, Lp - L), (0, 0), (0, 0)))

    def band(a):
        a = jnp.pad(stream(a), ((0, 0), (0, 0), (M, 0), (0, 0), (0, 0)))
        prev = a[:, :, :Lp].reshape(B, dil, nb, M, H, E)
        cur = a[:, :, M:].reshape(B, dil, nb, M, H, E)
        return jnp.concatenate([prev, cur], axis=3)

    qs = stream(q).reshape(B, dil, nb, M, H, E)
    kb, vb = band(k), band(v)
    s = jnp.einsum("bdnihe,bdnjhe->bdnhij", qs, kb) * (E ** -0.5)
    ii = jnp.arange(M)[None, :, None]
    jj = jnp.arange(2 * M)[None, None, :]
    blk = jnp.arange(nb)[:, None, None]
    off = ii + M - jj
    valid = (off >= 0) & (off <= M) & (blk * M - M + jj >= 0)
    s = jnp.where(valid[None, None, :, None, :, :], s, -jnp.inf)
    m = jnp.max(s, axis=-1, keepdims=True)
    p = jnp.exp(s - m)
    l = jnp.sum(p, axis=-1)
    o = jnp.einsum("bdnhij,bdnjhe->bdnihe", p, vb) / jnp.transpose(l, (0, 1, 2, 4, 3))[..., None]
    lse = jnp.transpose(m[..., 0] + jnp.log(l), (0, 1, 2, 4, 3))

    def unstream(a):
        a = a.reshape((B, dil, Lp) + a.shape[4:])[:, :, :L]
        return jnp.swapaxes(a, 1, 2).reshape((B, T) + a.shape[3:])

    return unstream(o), unstream(lse)


def dilated_sample(q, k, v, kbuf, vbuf, dil):
    Bd, S, H, E = q.shape
    M = ATT_STREAM_WIN
    Wb = kbuf.shape[1]
    kc = jnp.concatenate([kbuf, k], axis=1)
    vc = jnp.concatenate([vbuf, v], axis=1)
    idx = Wb + jnp.arange(S)[:, None] - dil * jnp.arange(M + 1)[None, :]
    valid = idx >= 0
    idx = jnp.maximum(idx, 0)
    kg, vg = kc[:, idx], vc[:, idx]
    s = jnp.einsum("bshe,bsmhe->bshm", q, kg) * (E ** -0.5)
    s = jnp.where(valid[None, :, None, :], s, -jnp.inf)
    m = jnp.max(s, axis=-1, keepdims=True)
    p = jnp.exp(s - m)
    l = jnp.sum(p, axis=-1)
    o = jnp.einsum("bshm,bsmhe->bshe", p, vg) / l[..., None]
    return o, m[..., 0] + jnp.log(l)


def token_mixer(xn, pos, ret_s0, kv_bufs, w_in, ret_gn_g, w_pa, w_pb, w_o):
    B, T, _ = xn.shape
    f32 = jnp.float32
    proj = xn @ w_in
    q_r, k_r, v_r, g_r, q_a, k_a, v_a, gate_a, gate_b = jnp.split(proj, IN_SPLIT_IDX, axis=-1)
    qr = rotary(q_r.reshape(B, T, RET_HEADS, RET_DK).astype(f32), pos)
    kr = rotary(k_r.reshape(B, T, RET_HEADS, RET_DK).astype(f32), pos) * (RET_DK ** -0.5)
    vr = v_r.reshape(B, T, RET_HEADS, RET_DV).astype(f32)
    s0 = jnp.zeros((B, RET_HEADS, RET_DK, RET_DV), f32) if ret_s0 is None else ret_s0.astype(f32)
    chunk = T if T <= RET_CHUNK else math.gcd(T, RET_CHUNK)
    o_r, s_new = retention(qr, kr, vr, s0, chunk)
    o_r = o_r * lax.rsqrt(jnp.mean(o_r * o_r, axis=-1, keepdims=True) + EPS)
    o_r = o_r.reshape(B, T, RET_V) * ret_gn_g.astype(f32) * jax.nn.silu(g_r.astype(f32))
    branch_a = o_r.astype(xn.dtype) @ w_pa
    qa = q_a.reshape(B, T, ATT_HEADS, ATT_HD)
    ka = k_a.reshape(B, T, ATT_HEADS, ATT_HD)
    va = v_a.reshape(B, T, ATT_HEADS, ATT_HD)
    outs, lses, new_kv = [], [], []
    for gi, (win, dil) in enumerate(ATT_GROUPS):
        hs = slice(gi * ATT_HPG, (gi + 1) * ATT_HPG)
        qg, kg, vg = qa[:, :, hs], ka[:, :, hs], va[:, :, hs]
        kv_rows = jnp.stack([kg, vg], axis=2)
        if kv_bufs is None:
            o, lse = dilated_prompt(qg.astype(f32), kg.astype(f32), vg.astype(f32), dil)
            new_kv.append(kv_rows[:, T - min(win, T):])
        else:
            buf = kv_bufs[gi].astype(f32)
            o, lse = dilated_sample(qg.astype(f32), kg.astype(f32), vg.astype(f32),
                                    buf[:, :, 0], buf[:, :, 1], dil)
            new_kv.append(kv_rows)
        outs.append(o)
        lses.append(lse)
    alpha = jax.nn.softmax(jnp.stack(lses, axis=0), axis=0)[..., None]
    o_a = jnp.sum(alpha * jnp.stack(outs, axis=0), axis=0).reshape(B, T, ATT_OUT)
    branch_b = o_a.astype(xn.dtype) @ w_pb
    merged = jax.nn.sigmoid(gate_a) * branch_a + jax.nn.sigmoid(gate_b) * branch_b
    return merged @ w_o, s_new, new_kv


def swiglu(xn, w_ffn_in, w_ffn_out):
    a, b = jnp.split(xn @ w_ffn_in, 2, axis=-1)
    return (jax.nn.silu(a) * b) @ w_ffn_out


def run_trunk(x, pos, ret_states, kv_caches, ln1_g, w_in, ret_gn_g, w_pa, w_pb, w_o,
              ln2_g, w_ffn_in, w_ffn_out, lnf_g):
    h = x
    new_s = []
    new_kv = [[] for _ in range(N_GROUPS)]
    for layer in range(DEPTH):
        s0 = None if ret_states is None else ret_states[layer]
        bufs = None if kv_caches is None else [c[layer] for c in kv_caches]
        mix, s_new, kvs = token_mixer(rmsnorm(h, ln1_g[layer]), pos, s0, bufs, w_in[layer],
                                      ret_gn_g[layer], w_pa[layer], w_pb[layer], w_o[layer])
        h = h + mix
        h = h + swiglu(rmsnorm(h, ln2_g[layer]), w_ffn_in[layer], w_ffn_out[layer])
        new_s.append(s_new.astype(x.dtype))
        for gi in range(N_GROUPS):
            new_kv[gi].append(kvs[gi])
    y = rmsnorm(h, lnf_g)
    return y, jnp.stack(new_s, axis=0), [jnp.stack(a, axis=0) for a in new_kv]


def setup_inputs(seed: int = 0) -> dict:
    key = jax.random.key(seed)
    ks = jax.random.split(key, 16)
    nrm = jax.random.normal
    f32 = jnp.float32
    return {
        "x_prompt": nrm(ks[0], (BATCH, SEQ, D_MODEL), f32),
        "x_sample": nrm(ks[1], (DEC_BATCH, DEC_SEQ, D_MODEL), f32),
        "state_ret": 0.1 * nrm(ks[2], (DEPTH, DEC_BATCH, RET_HEADS, RET_DK, RET_DV), f32),
        "cache_kv_w128": nrm(ks[3], (DEPTH, DEC_BATCH, min(ATT_GROUPS[0][0], PAST_LEN), 2, ATT_HPG, ATT_HD), f32),
        "cache_kv_w512": nrm(ks[4], (DEPTH, DEC_BATCH, min(ATT_GROUPS[1][0], PAST_LEN), 2, ATT_HPG, ATT_HD), f32),
        "cache_kv_w2048": nrm(ks[5], (DEPTH, DEC_BATCH, min(ATT_GROUPS[2][0], PAST_LEN), 2, ATT_HPG, ATT_HD), f32),
        "ln1_g": 1.0 + 0.02 * nrm(ks[6], (DEPTH, D_MODEL), f32),
        "w_in": nrm(ks[7], (DEPTH, D_MODEL, IN_TOTAL), f32) * D_MODEL ** -0.5,
        "ret_gn_g": 1.0 + 0.02 * nrm(ks[8], (DEPTH, RET_V), f32),
        "w_pa": nrm(ks[9], (DEPTH, RET_V, D_MODEL), f32) * RET_V ** -0.5,
        "w_pb": nrm(ks[10], (DEPTH, ATT_OUT, D_MODEL), f32) * ATT_OUT ** -0.5,
        "w_o": nrm(ks[11], (DEPTH, D_MODEL, D_MODEL), f32) * D_MODEL ** -0.5,
        "ln2_g": 1.0 + 0.02 * nrm(ks[12], (DEPTH, D_MODEL), f32),
        "w_ffn_in": nrm(ks[13], (DEPTH, D_MODEL, 2 * FFN_HIDDEN), f32) * D_MODEL ** -0.5,
        "w_ffn_out": nrm(ks[14], (DEPTH, FFN_HIDDEN, D_MODEL), f32) * FFN_HIDDEN ** -0.5,
        "lnf_g": 1.0 + 0.02 * nrm(ks[15], (D_MODEL,), f32),
    }


def reference(x_prompt, x_sample, state_ret, cache_kv_w128, cache_kv_w512, cache_kv_w2048,
              ln1_g, w_in, ret_gn_g, w_pa, w_pb, w_o, ln2_g, w_ffn_in, w_ffn_out, lnf_g):
    pos_p = jnp.arange(x_prompt.shape[1], dtype=jnp.int32)
    pos_s = PAST_LEN + jnp.arange(x_sample.shape[1], dtype=jnp.int32)
    y_prompt, s_p, kv_p = run_trunk(x_prompt, pos_p, None, None, ln1_g, w_in, ret_gn_g, w_pa, w_pb,
                                    w_o, ln2_g, w_ffn_in, w_ffn_out, lnf_g)
    y_sample, s_s, kv_s = run_trunk(x_sample, pos_s, state_ret,
                                    [cache_kv_w128, cache_kv_w512, cache_kv_w2048],
                                    ln1_g, w_in, ret_gn_g, w_pa, w_pb, w_o, ln2_g,
                                    w_ffn_in, w_ffn_out, lnf_g)
    return (y_prompt, y_sample, s_p, s_s, kv_p[0], kv_s[0], kv_p[1], kv_s[1], kv_p[2], kv_s[2])
```

```python
import math
import os
KDBG = os.environ.get('KDBG', '')
from contextlib import ExitStack
import numpy as np
import concourse.bass as bass
import concourse.mybir as mybir
from concourse.bass_utils import run_bass_kernel_spmd

F32 = mybir.dt.float32
BF16 = mybir.dt.bfloat16
AF = mybir.ActivationFunctionType
ALU = mybir.AluOpType

ENGS = ["pe", "act", "dve", "pool", "sp"]

D = 1024
SEQ = 4096
NCH = SEQ // 128
PAST = 16384
EPS = 1e-6
NEG = -30000.0
GROUPS = ((128, 1), (512, 4), (2048, 16))
RING = (2, 5, 17)
FF = 2816
WS = 132
NPRE = 16
ATT_SCALE = 128.0 ** -0.5


class Prog:
    def __init__(self, nc, es):
        self.nc = nc
        self.es = es
        self.ops = {e: [] for e in ENGS}
        self.cnt = {e: 0 for e in ENGS}
        self.known = {e: {} for e in ENGS}
        self.lastw = {}
        self.reads = {}
        self.sems = {e: es.enter_context(nc.semaphore("c_" + e)) for e in ENGS}
        self.dsem = {}
        self.dval = {}
        self.excl = set()

    def _deps(self, eng, reads, writes):
        toks = []
        for r in reads:
            t = self.lastw.get(r)
            if t is not None:
                toks.append(t)
        for w in writes:
            t = self.lastw.get(w)
            if t is not None:
                toks.append(t)
            toks.extend(self.reads.get(w, []))
        waits = {}
        for (kind, key, val) in toks:
            if kind == "eng" and key == eng and eng == "pe":
                continue
            k = (kind, key)
            if self.known[eng].get(k, 0) >= val:
                continue
            if waits.get(k, 0) < val:
                waits[k] = val
        out = []
        for (kind, key), val in waits.items():
            self.known[eng][(kind, key)] = val
            sem = self.sems[key] if kind == "eng" else self.dsem[key]
            out.append((sem, val))
        return out

    def _register(self, tok, reads, writes):
        for r in reads:
            self.reads.setdefault(r, []).append(tok)
        for w in writes:
            self.lastw[w] = tok
            self.reads[w] = []

    def emit(self, eng, fn, reads=(), writes=(), inc=True):
        if self.excl:
            writes = list(writes) + [r for r in reads if r in self.excl]
            reads = [r for r in reads if r not in self.excl]
        waits = self._deps(eng, reads, writes)
        if inc:
            self.cnt[eng] += 1
            tok = ("eng", eng, self.cnt[eng])
        else:
            assert eng == "pe"
            tok = ("eng", eng, self.cnt[eng] + 1)
        self._register(tok, reads, writes)
        sem = self.sems[eng]

        def run(e, waits=waits, fn=fn, sem=sem, inc=inc):
            for (s, v) in waits:
                e.wait_ge(s, v)
            ins = fn(e)
            if inc:
                ins.then_inc(sem, 1)

        self.ops[eng].append(run)

    def dma(self, eng, key, out, in_, reads=(), writes=(), **kw):
        if key not in self.dsem:
            self.dsem[key] = self.es.enter_context(self.nc.semaphore("d_" + str(key)))
            self.dval[key] = 0
        waits = self._deps(eng, reads, writes)
        self.dval[key] += 16
        tok = ("dma", key, self.dval[key])
        self._register(tok, reads, writes)
        sem = self.dsem[key]

        def run(e, waits=waits, sem=sem, out=out, in_=in_, kw=kw):
            for (s, v) in waits:
                e.wait_ge(s, v)
            e.dma_start(out=out, in_=in_, **kw).then_inc(sem, 16)

        self.ops[eng].append(run)

    def wait_all_dma(self, eng):
        ws = []
        for k, v in self.dval.items():
            if self.known[eng].get(("dma", k), 0) < v:
                ws.append((self.dsem[k], v))
                self.known[eng][("dma", k)] = v

        def run(e, ws=ws):
            for (s, v) in ws:
                e.wait_ge(s, v)

        self.ops[eng].append(run)

    def finish(self, block):
        ops = self.ops

        @block.tensor
        def _(e):
            for f in ops["pe"]:
                f(e)

        @block.scalar
        def _(e):
            for f in ops["act"]:
                f(e)

        @block.vector
        def _(e):
            for f in ops["dve"]:
                f(e)

        @block.gpsimd
        def _(e):
            for f in ops["pool"]:
                f(e)

        @block.sync
        def _(e):
            for f in ops["sp"]:
                f(e)


class StopBuild(Exception):
    pass


def build_program(nchunks=NCH, with_sample=True, stop=99, npre=NPRE):
    def chk(n):
        if n > stop:
            raise StopBuild()

    nc = bass.Bass("TRN2", target_bir_lowering=False)

    def din(name, shape):
        return nc.dram_tensor(name, list(shape), F32, kind="ExternalInput").ap()

    def dout(name, shape):
        return nc.dram_tensor(name, list(shape), F32, kind="ExternalOutput").ap()

    xp = din("xp", [SEQ, D])
    xs = din("xs", [4, D])
    st0 = din("st0", [4, 4, 256, 512])
    caches = [din("c%d" % g, [4, GROUPS[g][0], 2, 4, 128]) for g in range(3)]
    w_in = din("w_in", [D, 12800])
    w_pa = din("w_pa", [2048, D])
    w_pb = din("w_pb", [512, D])
    w_o = din("w_o", [D, D])
    w_fi = din("w_fi", [D, 2 * FF])
    w_fo = din("w_fo", [FF, D])
    ln1 = din("ln1", [128, 8])
    ln2 = din("ln2", [128, 8])
    lnf = din("lnf", [128, D])
    gng = din("gng", [128, 16])
    cs_tab = din("cs_tab", [NCH, 128, 2, WS])
    retmask_d = din("retmask", [4, 128, 128])
    attmask_d = din("attmask", [8, 128, 128])
    attmask_pd = din("attmask_p", [8, 128, 128])
    cols_d = din("cols", [128, 16])
    qmask_d = din("qmask", [128, 16])
    oh4_d = din("oh4", [4, 4])

    yp = dout("yp", [SEQ - npre * 128, D])
    ys = dout("ys", [4, D])
    sp_o = dout("sp_o", [4, 256, 512])
    ss_o = dout("ss_o", [4, 4, 256, 512])
    kvp = [dout("kvp%d" % g, [GROUPS[g][0], 2, 4, 128]) for g in range(3)]
    kvs = [dout("kvs%d" % g, [4, 2, 4, 128]) for g in range(3)]

    gam = [1.0 - 2.0 ** (-5.0 - h) for h in range(4)]
    cdec = [g ** 128 for g in gam]

    with ExitStack() as es:
        def sb(name, shape, dt):
            return es.enter_context(nc.sbuf_tensor(name, list(shape), dt))

        def ps(name, shape, dt):
            return es.enter_context(nc.psum_tensor(name, list(shape), dt))

        xh = sb("xh", [128, 2, D], F32)
        xn_tmp = sb("xn_tmp", [128, D], BF16)
        xnT = sb("xnT", [128, 8, 256], BF16)
        hnT = sb("hnT", [128, 8, 256], BF16)
        q_rT = sb("q_rT", [128, 8, 256], BF16)
        k_rT = sb("k_rT", [128, 8, 256], BF16)
        v_r = sb("v_r", [128, 2, 2048], BF16)
        g_s = sb("g_s", [128, 2, 2048], BF16)
        S = sb("S", [128, 8, 512], F32)
        S_bf = sb("S_bf", [128, 8, 512], BF16)
        o_rT = sb("o_rT", [128, 16, WS], BF16)
        ga_s = sb("ga_s", [128, 8, 256], BF16)
        gb_s = sb("gb_s", [128, 8, 256], BF16)
        mergedT = sb("mergedT", [128, 8, WS], BF16)
        q_aT = sb("q_aT", [128, 12, WS], BF16)
        KT = [sb("KT%d" % g, [128, 4, RING[g] * 128], BF16) for g in range(3)]
        VR = [sb("VR%d" % g, [128, RING[g], 512], BF16) for g in range(3)]
        o_aT = sb("o_aT", [128, 4, WS], BF16)
        actT = sb("actT", [128, 22, 256], BF16)
        wbuf = [sb("wbuf%d" % i, [128, 4096], BF16) for i in range(3)]
        cs = sb("cs", [128, 2, 256], F32)
        retmask = sb("retmask_s", [128, 4, 128], F32)
        attmask = sb("attmask_s", [128, 8, 128], BF16)
        attmask_p = sb("attmask_ps", [128, 8, 128], BF16)
        lnf_bc = sb("lnf_bc", [128, D], F32)
        ident = sb("ident", [128, 128], BF16)
        identf = sb("identf", [128, 128], F32)
        ones = sb("ones", [128, 128], BF16)
        g1T = sb("g1T", [128, 8], F32)
        g2T = sb("g2T", [128, 8], F32)
        gnT = sb("gnT", [128, 16], F32)
        colt = sb("colt", [128, 16], F32)
        qmask = sb("qmask_s", [128, 16], BF16)
        oh4 = sb("oh4_s", [4, 4], F32)
        stat = sb("stat", [128, 8], F32)
        rt = [sb("rt%d" % i, [128, 256], F32) for i in range(4)]
        scmT = sb("scmT", [128, 128], BF16)
        kd = sb("kd", [128, 4, 256], BF16)
        o_g = sb("o_g", [128, 512], BF16)
        PT = sb("PT", [128, 4, 128], BF16)
        PT2 = sb("PT2", [128, 4, 128], BF16)
        rZ = sb("rZ", [128, 128], F32)
        kvst = [sb("kvst%d" % i, [128, 512], F32) for i in range(2)]
        tmpf = sb("tmpf", [128, WS], F32)
        sa_t = sb("sa_t", [128, 256], BF16)
        s0t = sb("s0t", [128, 2, 512], F32)
        snew_bf = sb("snew_bf", [128, 2, 512], BF16)
        qm = sb("qm", [128, 2, 4], BF16)
        Kc = sb("Kc", [128, 512], BF16)
        Vc = sb("Vc", [128, 512], BF16)
        KTs = sb("KTs", [128, 4, 128], BF16)
        PTs = sb("PTs", [128, 4], BF16)
        ksT = sb("ksT", [128, 12, 4], BF16)
        vs_a = sb("vs_a", [4, 3, 512], BF16)
        Ef = sb("Ef", [4, 4], F32)
        Dm = sb("Dm", [4, 4], BF16)

        pT = ps("pT", [128, 8, 128], BF16)
        pf = [ps("pf%d" % i, [128, 2, 256], F32) for i in range(2)]
        pt = [ps("pt%d" % i, [128, 512], F32) for i in range(2)]
        psc = ps("psc", [128, 4, 128], F32)
        pS = ps("pS", [128, 512], F32)
        pUZ = ps("pUZ", [128, 4, 128], F32)

        P = Prog(nc, es)
        P.excl.update(["pT", ("pf", 0), ("pf", 1), ("pt", 0), ("pt", 1), "psc", "pS", "pUZ"])

        def mm(out, lhsT, rhs, start, stop, reads, writes, inc=None, zacc=False):
            if inc is None:
                inc = stop
            if zacc:
                P.emit("pe", lambda e: e.matmul(out, lhsT=lhsT, rhs=rhs, start=start, stop=stop,
                                                skip_group_check=True), reads, writes, inc)
            else:
                P.emit("pe", lambda e: e.matmul(out, lhsT=lhsT, rhs=rhs, start=start, stop=stop),
                       reads, writes, inc)

        def tr(out, in_, idn, reads, writes, inc=True):
            P.emit("pe", lambda e: e.transpose(out=out, in_=in_, identity=idn), reads, writes, inc)

        def act(out, in_, func, reads, writes, scale=None, bias=None, accum=None):
            kw = {}
            if scale is not None:
                kw["scale"] = scale
            if bias is not None:
                kw["bias"] = bias
            if accum is not None:
                kw["accum_out"] = accum
            P.emit("act", lambda e: e.activation(out=out, in_=in_, func=func, **kw), reads, writes)

        def tt(out, in0, in1, op, reads, writes, eng="dve"):
            P.emit(eng, lambda e: e.tensor_tensor(out=out, in0=in0, in1=in1, op=op), reads, writes)

        def tsc(out, in0, s1, op0, reads, writes, s2=None, op1=None, eng="dve"):
            if op1 is None:
                P.emit(eng, lambda e: e.tensor_scalar(out=out, in0=in0, scalar1=s1, scalar2=None, op0=op0),
                       reads, writes)
            else:
                P.emit(eng, lambda e: e.tensor_scalar(out=out, in0=in0, scalar1=s1, scalar2=s2, op0=op0, op1=op1),
                       reads, writes)

        def stt(out, in0, scalar, in1, op0, op1, reads, writes):
            P.emit("dve", lambda e: e.scalar_tensor_tensor(out=out, in0=in0, scalar=scalar, in1=in1,
                                                           op0=op0, op1=op1), reads, writes)

        def recip(out, in_, reads, writes):
            P.emit("dve", lambda e: e.reciprocal(out=out, in_=in_), reads, writes)

        def cp(out, in_, reads, writes, eng="dve"):
            P.emit(eng, lambda e: e.tensor_copy(out=out, in_=in_), reads, writes)

        P.dma("sp", "k_ret", retmask[:], retmask_d.rearrange("h j i -> j h i"), writes=["retmask"])
        P.dma("pool", "k_att", attmask[:], attmask_d.rearrange("m j i -> j m i"), writes=["attmask"])
        P.dma("pool", "k_attp", attmask_p[:], attmask_pd.rearrange("m j i -> j m i"), writes=["attmask_p"])
        P.dma("sp", "k_lnf", lnf_bc[:], lnf[:, :], writes=["lnf_bc"])
        P.dma("sp", "k_g1", g1T[:], ln1[:, :], writes=["g1T"])
        P.dma("sp", "k_g2", g2T[:], ln2[:, :], writes=["g2T"])
        P.dma("sp", "k_gn", gnT[:], gng[:, :], writes=["gnT"])
        P.dma("sp", "k_col", colt[:], cols_d[:, :], writes=["colt"])
        P.dma("pool", "k_qm", qmask[:], qmask_d[:, :], writes=["qmask"])
        P.dma("sp", "k_oh", oh4[:], oh4_d[:, :], writes=["oh4"])
        P.emit("pool", lambda e: e.memset(identf[:], 0.0), writes=["identf"])
        P.emit("pool", lambda e: e.affine_select(out=identf[:], in_=identf[:], pattern=[[-1, 128]],
                                                 compare_op=ALU.not_equal, fill=1.0, base=0,
                                                 channel_multiplier=1),
               reads=["identf"], writes=["identf"])
        cp(ident[:], identf[:], ["identf"], ["ident"])
        P.emit("pool", lambda e: e.memset(ones[:], 1.0), writes=["ones"])

        wq = []
        wstate = {"issued": 0, "cur": -1}

        def w_issue_upto(n):
            while wstate["issued"] <= n and wstate["issued"] < len(wq):
                i = wstate["issued"]
                slot = i % 3
                j = wq[i]
                ncols = 2816 if j >= 43 else 4096
                P.dma("sp", "w%d" % slot, wbuf[slot][:, 0:ncols], scr[j][:, 0:ncols],
                      reads=[("scr", j, 0), ("scr", j, 1)], writes=[("wbuf", slot, 0), ("wbuf", slot, 1)])
                wstate["issued"] += 1

        def w_next():
            wstate["cur"] += 1
            n = wstate["cur"]
            w_issue_upto(n + 2)
            return wbuf[n % 3], [("wbuf", n % 3, 0), ("wbuf", n % 3, 1)]

        def v3(buf, a, b):
            return buf[:, 0:a * b].rearrange("p (a b) -> p a b", b=b)

        def unit_win(u):
            return lambda buf: [(v3(buf, 8, 512), w_in[:, 512 * u:512 * (u + 1)].rearrange("(kc p) n -> p kc n", p=128))]

        def unit_wpa(u):
            return lambda buf: [(v3(buf, 16, 256), w_pa[:, 256 * u:256 * (u + 1)].rearrange("(kc p) n -> p kc n", p=128))]

        def unit_wpb():
            return lambda buf: [(v3(buf, 4, 1024), w_pb[:, :].rearrange("(kc p) n -> p kc n", p=128))]

        def unit_wo(u):
            return lambda buf: [(v3(buf, 8, 512), w_o[:, 512 * u:512 * (u + 1)].rearrange("(kc p) n -> p kc n", p=128))]

        def unit_fi(u):
            def f(buf):
                va = buf[:, 0:2048].rearrange("p (a b) -> p a b", b=256)
                vb = buf[:, 2048:4096].rearrange("p (a b) -> p a b", b=256)
                return [(va, w_fi[:, 256 * u:256 * (u + 1)].rearrange("(kc p) n -> p kc n", p=128)),
                        (vb, w_fi[:, FF + 256 * u:FF + 256 * (u + 1)].rearrange("(kc p) n -> p kc n", p=128))]
            return f

        def unit_fo(hh, q):
            return lambda buf: [(v3(buf, 11, 256), w_fo[hh * 1408:(hh + 1) * 1408, 256 * q:256 * (q + 1)].rearrange("(kc p) n -> p kc n", p=128))]

        def iter_has_kout(c, g):
            return c >= NCH - GROUPS[g][0] // 128

        unit_defs = ([unit_win(u) for u in range(25)] + [unit_wpa(u) for u in range(4)] + [unit_wpb()]
                     + [unit_wo(u) for u in range(2)] + [unit_fi(u) for u in range(11)]
                     + [unit_fo(hh, q) for hh in range(2) for q in range(4)])
        NU = len(unit_defs)
        assert NU == 51
        scr = nc.dram_tensor("wscr", [NU, 128, 4096], BF16, kind="Internal").ap()
        for j, ud in enumerate(unit_defs):
            parts = ud(scr[j])
            for pi, (dst, src) in enumerate(parts):
                wr = [("scr", j, 0), ("scr", j, 1)] if len(parts) == 1 else [("scr", j, pi)]
                P.dma("pool", "cv%d" % j, dst, src, writes=wr)
        SHARED_UNITS = [0, 1, 2, 3, 4, 5, 6, 7, 8, 9, 10, 11, 21, 22, 23, 24]

        def units_for(c):
            if c >= npre:
                return list(range(32))
            need = [g for g in range(3) if c >= npre - GROUPS[g][0] // 128]
            return [2, 3, 4, 5, 6, 7] + [15 + g for g in need] + [18 + g for g in need]

        plan = []
        c_ = 0
        while c_ < nchunks:
            if c_ + 1 < npre and c_ + 1 < nchunks:
                plan.append(("body2", c_, c_ + 1))
                c_ += 2
            elif c_ < npre:
                plan.append(("body", c_, 0))
                c_ += 1
            elif c_ == npre or c_ == nchunks - 1:
                plan.append(("body", c_, 0))
                fl_ = [(0, 128, 0, "p", c_ - npre)]
                if with_sample and c_ == npre:
                    fl_.append((1, 4, 128, "s", 0))
                plan.append(("ffn", fl_))
                c_ += 1
            else:
                plan.append(("shared", c_, c_ + 1))
                plan.append(("rest", c_, 0))
                plan.append(("rest", c_ + 1, 1))
                plan.append(("ffn", [(0, 128, 0, "p", c_ - npre), (1, 128, 128, "p", c_ + 1 - npre)]))
                c_ += 2
        for it in plan:
            if it[0] == "body":
                wq.extend(units_for(it[1]))
            elif it[0] == "body2":
                wq.extend(units_for(it[2]))
            elif it[0] == "shared":
                wq.extend(SHARED_UNITS)
            elif it[0] == "rest":
                wq.extend([u_ for u_ in range(32) if u_ not in SHARED_UNITS])
            else:
                wq.extend(range(32, NU))

        def norm_T(k, R, col0, gT, gkey, dstT, dkey):
            act(xn_tmp[:R, :], xh[:R, k, :], AF.Square, [("xh", k)], [("xn_tmp", 0), ("xn_tmp", 1), ("stat", k)],
                accum=stat[:R, k:k + 1])
            act(stat[:R, 2 + k:3 + k], stat[:R, k:k + 1], AF.Sqrt, [("stat", k), "colt"], [("stat", 2 + k)],
                scale=1.0 / D, bias=colt[:R, 8:9])
            recip(stat[:R, 4 + k:5 + k], stat[:R, 2 + k:3 + k], [("stat", 2 + k)], [("stat", 4 + k)])
            for hf in range(2):
                tsc(xn_tmp[:R, hf * 512:(hf + 1) * 512], xh[:R, k, hf * 512:(hf + 1) * 512], stat[:R, 4 + k:5 + k],
                    ALU.mult, [("xh", k), ("stat", 4 + k)], [("xn_tmp", hf)])
            for kc in range(8):
                tr(pT[:, kc, 0:R], xn_tmp[:R, kc * 128:(kc + 1) * 128], ident[:R, :R],
                   [("xn_tmp", kc // 4), "ident"], ["pT"], inc=(kc == 7))
            for kc in range(8):
                if kc % 2 == 0:
                    act(dstT[:, kc, col0:col0 + R], pT[:, kc, 0:R], AF.Copy, ["pT", gkey], [dkey],
                        scale=gT[:, kc:kc + 1])
                else:
                    tsc(dstT[:, kc, col0:col0 + R], pT[:, kc, 0:R], gT[:, kc:kc + 1], ALU.mult, ["pT", gkey], [dkey])

        pfi = [0]
        pti = [0]

        def next_pf():
            pfi[0] ^= 1
            return pf[pfi[0]], ("pf", pfi[0])

        def next_pt():
            pti[0] ^= 1
            return pt[pti[0]], ("pt", pti[0])

        kvi = [0]

        def next_kvst():
            kvi[0] ^= 1
            return kvst[kvi[0]], ("kvst", kvi[0])

        def body(c, kx, c2=None, phase=None, pk0=0):
            prefix = c < npre
            pair = c2 is not None
            sample = with_sample and c == npre and phase is None
            ulist = units_for(c2 if pair else c)
            if phase == "shared":
                ulist = SHARED_UNITS
            elif phase == "rest":
                ulist = [u_ for u_ in range(32) if u_ not in SHARED_UNITS]
            W = 256 if pair else (WS if sample else 128)
            chunks = [(0, 128, 0)] + ([(1, 4, 128)] if sample else [])
            if pair:
                chunks = [(0, 128, 0), (1, 128, 128)]
            if phase == "rest":
                chunks = [(pk0, 128, 0)]
            sc = 128 * pk0 if phase == "rest" else 0
            pchunks = [t for t in chunks if t[1] == 128]
            cids = {0: c, 1: c2}
            if phase == "rest":
                cids = {pk0: c}
            xs_of = {0: kx, 1: 1}
            cskeys = [("cs", 0), ("cs", 1)]

            if phase != "rest":
                P.dma("pool", "x%d" % kx, xh[:, kx, :], xp[c * 128:(c + 1) * 128, :], writes=[("xh", kx)])
            if phase == "rest":
                pass
            elif pair:
                P.dma("pool", "x1", xh[:, 1, :], xp[c2 * 128:(c2 + 1) * 128, :], writes=[("xh", 1)])
                P.dma("pool", "cs0", cs[:, :, 0:128], cs_tab[c][:, :, 0:128], writes=[("cs", 0)])
                P.dma("pool", "cs1", cs[:, :, 128:256], cs_tab[c2][:, :, 0:128], writes=[("cs", 1)])
            else:
                P.dma("pool", "cs0", cs[:, :, 0:W], cs_tab[c][:, :, 0:W], writes=cskeys)
            if sample:
                P.dma("pool", "x1", xh[:4, 1, :], xs[:, :], writes=[("xh", 1)])
            chk(1)
            for (k, R, col0) in (chunks if phase != "rest" else []):
                norm_T(xs_of[k], R, col0, g1T, "g1T", xnT, "xnT")

            chk(2)
            def feat_unit(wb, wkey, nf, evac):
                for half in range(nf // 2):
                    ptile, pkey = next_pf()
                    for fl in range(2):
                        f = half * 2 + fl
                        for kc in range(8):
                            if 'nomm' in KDBG:
                                continue
                            mm(ptile[:, fl, 0:W], v3(wb, 8, 512)[:, kc, f * 128:(f + 1) * 128], xnT[:, kc, sc:sc + W],
                               kc == 0, kc == 7, wkey + ["xnT"], [pkey])
                    if 'noevac' not in KDBG:
                        evac(half, ptile, pkey)

            def tok_unit(wb, wkey, evac):
                for (k, R, col0) in chunks:
                    ptile, pkey = next_pt()
                    for kc in range(8):
                        mm(ptile[:R, :], xnT[:, kc, sc + col0:sc + col0 + R], v3(wb, 8, 512)[:, kc, :],
                           kc == 0, kc == 7, wkey + ["xnT"], [pkey])
                    evac(k, R, ptile, pkey)

            for u in range(25):
                chk(2 + (u + 1) / 100.0)
                if u not in ulist:
                    continue
                wb, wkey = w_next()
                if u < 4:
                    dstT, dkey = (q_rT, "q_rT") if u < 2 else (k_rT, "k_rT")
                    hbase = (u % 2) * 2

                    def evac_rot(half, ptile, pkey, dstT=dstT, dkey=dkey, hbase=hbase):
                        h = hbase + half
                        A = ptile[:, 0, 0:W]
                        B = ptile[:, 1, 0:W]
                        cosv = cs[:, 0, 0:W]
                        sinv = cs[:, 1, 0:W]
                        tt(rt[0][:, 0:W], A, cosv, ALU.mult, [pkey] + cskeys, [("rt", 0)])
                        tt(rt[1][:, 0:W], B, sinv, ALU.mult, [pkey] + cskeys, [("rt", 1)])
                        tt(rt[2][:, 0:W], A, sinv, ALU.mult, [pkey] + cskeys, [("rt", 2)])
                        tt(rt[3][:, 0:W], B, cosv, ALU.mult, [pkey] + cskeys, [("rt", 3)])
                        tt(dstT[:, 2 * h, 0:W], rt[0][:, 0:W], rt[1][:, 0:W], ALU.subtract,
                           [("rt", 0), ("rt", 1)], [dkey])
                        tt(dstT[:, 2 * h + 1, 0:W], rt[2][:, 0:W], rt[3][:, 0:W], ALU.add,
                           [("rt", 2), ("rt", 3)], [dkey])
                    feat_unit(wb, wkey, 4, evac_rot)
                elif u < 12:
                    isv = u < 8
                    ucol = (u - 4) % 4

                    def evac_vg(k, R, ptile, pkey, isv=isv, ucol=ucol):
                        if isv:
                            cp(v_r[:R, k, ucol * 512:(ucol + 1) * 512], ptile[:R, :], [pkey], [("v_r", k)])
                        else:
                            act(g_s[:R, k, ucol * 512:(ucol + 1) * 512], ptile[:R, :], AF.Silu, [pkey], [("g_s", k)])
                    tok_unit(wb, wkey, evac_vg)
                elif u < 15:
                    g = u - 12

                    def evac_qa(half, ptile, pkey, g=g):
                        act(q_aT[:, g * 4 + half * 2, 0:W], ptile[:, 0, 0:W], AF.Copy, [pkey], ["q_aT"])
                        cp(q_aT[:, g * 4 + half * 2 + 1, 0:W], ptile[:, 1, 0:W], [pkey], ["q_aT"])
                    feat_unit(wb, wkey, 4, evac_qa)
                elif u < 18:
                    g = u - 15

                    def evac_ka(half, ptile, pkey, g=g):
                        for fl in range(2):
                            h = half * 2 + fl
                            for (pk, pR, pcol) in pchunks:
                                slot = cids[pk] % RING[g]
                                if fl == 0:
                                    act(KT[g][:, h, slot * 128:(slot + 1) * 128], ptile[:, fl, pcol:pcol + 128],
                                        AF.Copy, [pkey], [("KT", g, slot)])
                                else:
                                    cp(KT[g][:, h, slot * 128:(slot + 1) * 128], ptile[:, fl, pcol:pcol + 128],
                                       [pkey], [("KT", g, slot)])
                            if sample:
                                act(ksT[:, g * 4 + h, 0:4], ptile[:, fl, 128:132], AF.Copy, [pkey], ["ksT"])
                    feat_unit(wb, wkey, 4, evac_ka)
                    if iter_has_kout(c, g) or sample:
                        def evac_kout(k, R, ptile, pkey, g=g):
                            if R == 128 and not iter_has_kout(cids[k], g):
                                return
                            st, skey = next_kvst()
                            cp(st[:R, :], ptile[:R, :], [pkey], [skey])
                            if R == 128:
                                r0 = cids[k] * 128 - (SEQ - GROUPS[g][0])
                                P.dma("sp", "st_" + str(skey[1]), kvp[g][r0:r0 + 128, 0, :, :],
                                      st[:, :].rearrange("p (h e) -> p h e", e=128), reads=[skey])
                            else:
                                P.dma("sp", "st_" + str(skey[1]), kvs[g][:, 0, :, :],
                                      st[:4, :].rearrange("p (h e) -> p h e", e=128), reads=[skey])
                        tok_unit(wb, wkey, evac_kout)
                elif u < 21:
                    g = u - 18

                    def evac_va(k, R, ptile, pkey, g=g):
                        if R == 128:
                            slot = cids[k] % RING[g]
                            act(VR[g][:, slot, :], ptile[:, :], AF.Copy, [pkey], [("VR", g, slot)])
                        elif 'nova1' not in KDBG:
                            act(vs_a[:4, g, :], ptile[:4, :], AF.Copy, [pkey], ["vs_a"])
                        if ((R == 128 and iter_has_kout(cids[k], g)) or R == 4) and 'nova2' not in KDBG:
                            st, skey = next_kvst()
                            cp(st[:R, :], ptile[:R, :], [pkey], [skey])
                            if R == 128:
                                r0 = cids[k] * 128 - (SEQ - GROUPS[g][0])
                                P.dma("sp", "st_" + str(skey[1]), kvp[g][r0:r0 + 128, 1, :, :],
                                      st[:, :].rearrange("p (h e) -> p h e", e=128), reads=[skey])
                            else:
                                P.dma("sp", "st_" + str(skey[1]), kvs[g][:, 1, :, :],
                                      st[:4, :].rearrange("p (h e) -> p h e", e=128), reads=[skey])
                    tok_unit(wb, wkey, evac_va)
                else:
                    dst, dkey = (ga_s, "ga_s") if u < 23 else (gb_s, "gb_s")
                    fbase = ((u - 21) % 2) * 4

                    def evac_gate(half, ptile, pkey, dst=dst, dkey=dkey, fbase=fbase):
                        for fl in range(2):
                            act(dst[:, fbase + half * 2 + fl, 0:W], ptile[:, fl, 0:W], AF.Sigmoid, [pkey], [dkey])
                    feat_unit(wb, wkey, 4, evac_gate)

            if phase == "shared":
                return
            chk(3)
            def gn_and_T(po, pokey, R, k, col0, h, epsc):
                gn_B(po, pokey, R, k, col0, h, epsc)
                gn_C(R, col0, h)

            def gn_B(po, pokey, R, k, col0, h, epsc):
                act(xn_tmp[:R, 0:512], po[:R, :], AF.Square, [pokey], [("xn_tmp", 0), ("stat", 6)], accum=stat[:R, 6:7])
                act(stat[:R, 7:8], stat[:R, 6:7], AF.Sqrt, [("stat", 6), "colt"], [("stat", 7)],
                    scale=1.0 / 512.0, bias=epsc)
                recip(stat[:R, 6:7], stat[:R, 7:8], [("stat", 7)], [("stat", 6)])
                stt(o_g[:R, :], po[:R, :], stat[:R, 6:7], g_s[:R, k, h * 512:(h + 1) * 512], ALU.mult, ALU.mult,
                    [pokey, ("stat", 6), ("g_s", k)], ["o_g"])

            pT2 = pf[1][:, :, :].bitcast(BF16)[:, 0, :].rearrange("p (a b) -> p a b", b=128)
            pT2key = ("pf", 1)

            def gn_C(R, col0, h):
                for vq in range(4):
                    tr(pT2[:, vq, 0:R], o_g[:R, vq * 128:(vq + 1) * 128], ident[:R, :R], ["o_g", "ident"], [pT2key],
                       inc=(vq == 3))
                for vq in range(4):
                    if vq % 2 == 0:
                        act(o_rT[:, h * 4 + vq, col0:col0 + R], pT2[:, vq, 0:R], AF.Copy, [pT2key, "gnT"], ["o_rT"],
                            scale=gnT[:, h * 4 + vq:h * 4 + vq + 1])
                    else:
                        tsc(o_rT[:, h * 4 + vq, col0:col0 + R], pT2[:, vq, 0:R], gnT[:, h * 4 + vq:h * 4 + vq + 1],
                            ALU.mult, [pT2key, "gnT"], ["o_rT"])

            for (pk, pR, pcol) in pchunks:
              cid = cids[pk]
              for h in range(4):
                for X in range(2):
                    tr(pT[:, X, :], k_rT[:, 2 * h + X, sc + pcol:sc + pcol + 128], ident[:, :], ["k_rT", "ident"], ["pT"], inc=(X == 1))
                act(kd[:, h, :], pT[:, 0:2, :].rearrange("p a b -> p (a b)"), AF.Copy, ["pT", "colt"], [("kd", h)],
                    scale=colt[:, h:h + 1])
                if not prefix:
                    for X in range(2):
                        mm(psc[:, 0, :], k_rT[:, 2 * h + X, sc + pcol:sc + pcol + 128], q_rT[:, 2 * h + X, sc:sc + 128], X == 0, X == 1,
                           ["k_rT", "q_rT"], ["psc"])
                    tt(scmT[:, :], psc[:, 0, :], retmask[:, h, :], ALU.mult, ["psc", "retmask"], ["scmT"])
                for X in range(2):
                    pb, pbkey = [(pS, "pS"), (pf[0][:, :, :].rearrange("p a b -> p (a b)"), ("pf", 0))][X]
                    mm(pb[:, :], kd[:, h, X * 128:(X + 1) * 128], v_r[:, pk, h * 512:(h + 1) * 512], True, True,
                       [("kd", h), ("v_r", pk)], [pbkey])
                    if cid == 0:
                        cp(S[:, 2 * h + X, :], pb[:, :], [pbkey], [("S", h)])
                    else:
                        stt(S[:, 2 * h + X, :], S[:, 2 * h + X, :], float(cdec[h]), pb[:, :], ALU.mult, ALU.add,
                            [pbkey, ("S", h)], [("S", h)])
                if not prefix:
                    po, pokey = next_pt()
                    last_is_intra = (cid == 0)
                    mm(po[:, :], scmT[:, :], v_r[:, pk, h * 512:(h + 1) * 512], True, last_is_intra,
                       ["scmT", ("v_r", pk)], [pokey])
                    if cid > 0:
                        for X in range(2):
                            mm(po[:, :], q_rT[:, 2 * h + X, sc:sc + 128], S_bf[:, 2 * h + X, :], False, X == 1,
                               ["q_rT", ("S_bf", h)], [pokey])
                if npre - 1 <= cid < nchunks - 1:
                    for X in range(2):
                        act(S_bf[:, 2 * h + X, :], S[:, 2 * h + X, :], AF.Copy, [("S", h)], [("S_bf", h)])
                if not prefix:
                    if h > 0:
                        gn_C(128, 0, h - 1)
                    gn_B(po, pokey, 128, pk, 0, h, colt[:, 4 + h:5 + h])
                    if h == 3:
                        gn_C(128, 0, 3)

            if prefix:
                return
            chk(3.5)
            if sample:
                for h in range(4):
                    for X in range(2):
                        tr(pT[:4, X, :], k_rT[:, 2 * h + X, 128:132], ident[:, :], ["k_rT", "ident"], ["pT"],
                           inc=(X == 1))
                    for b in range(4):
                        act(kd[:4, b, :], pT[:4, 0:2, :].rearrange("p a b -> p (a b)"), AF.Copy, ["pT", "colt"],
                            [("kd", b)], scale=colt[:4, 9 + b:10 + b])
                    po, pokey = next_pt()
                    for b in range(4):
                        P.dma("sp", "s0", s0t[:, :, :], st0[b, h].rearrange("(x p) v -> p x v", p=128),
                              writes=["s0t"])
                        for X in range(2):
                            tt(qm[:, X, :], q_rT[:, 2 * h + X, 128:132], qmask[:, 4 * b:4 * b + 4], ALU.mult,
                               ["q_rT", "qmask"], ["qm"])
                        for X in range(2):
                            mm(pS[:, :], kd[:4, b, X * 128:(X + 1) * 128], v_r[:4, 1, h * 512:(h + 1) * 512],
                               True, True, [("kd", b), ("v_r", 1)], ["pS"])
                            stt(s0t[:, X, :], s0t[:, X, :], float(gam[h]), pS[:, :], ALU.mult, ALU.add,
                                ["pS", "s0t"], ["s0t"])
                            act(snew_bf[:, X, :], s0t[:, X, :], AF.Copy, ["s0t"], ["snew_bf"])
                        P.dma("sp", "s0o", ss_o[b, h].rearrange("(x p) v -> p x v", p=128), s0t[:, :, :],
                              reads=["s0t"])
                        for X in range(2):
                            mm(po[:4, :], qm[:, X, :], snew_bf[:, X, :], b == 0 and X == 0, b == 3 and X == 1,
                               ["qm", "snew_bf"], [pokey])
                    gn_and_T(po, pokey, 4, 1, 128, h, colt[:4, 8:9])

            chk(4)
            for u in range(4):
                wb, wkey = w_next()
                ptile, pkey = next_pf()
                for fl in range(2):
                    for vc in range(16):
                        mm(ptile[:, fl, 0:W], v3(wb, 16, 256)[:, vc, fl * 128:(fl + 1) * 128], o_rT[:, vc, 0:W],
                           vc == 0, vc == 15, wkey + ["o_rT"], [pkey])
                for fl in range(2):
                    f = u * 2 + fl
                    tt(mergedT[:, f, 0:W], ptile[:, fl, 0:W], ga_s[:, f, sc:sc + W], ALU.mult, [pkey, "ga_s"],
                       [("mergedT", f)])

            chk(5)
            sc_tiles = [(psc, "psc"),
                        (pf[0][:, :, :].rearrange("p a (b c) -> p (a b) c", c=128), ("pf", 0)),
                        (pf[1][:, :, :].rearrange("p a (b c) -> p (a b) c", c=128), ("pf", 1))]
            pt_bufs = [(PT, "PT"), (PT2, "PT2")]
            uz_tiles = [(pUZ, "pUZ"), (pS[:, :].rearrange("p (a b) -> p a b", b=128), "pS")]

            def att_head(s, groups, nblk, uz, uzkey):
                def memz(e, uz=uz):
                    return e.memset(uz[:, 0:2, :], 0.0)
                P.emit("dve", memz, writes=[uzkey])

                def emit_S(i):
                    tile, tkey = sc_tiles[i % 3]
                    grp = groups[i]
                    for q, (g, slot, mi) in enumerate(grp):
                        mm(tile[:, q, :], KT[g][:, s, slot * 128:(slot + 1) * 128], q_aT[:, g * 4 + s, 0:128],
                           True, False, [("KT", g, slot), "q_aT"], [tkey], inc=False)
                        mtile, mkey = (attmask_p, "attmask_p") if mi[1] else (attmask, "attmask")
                        mm(tile[:, q, :], ident[:, :], mtile[:, mi[0], :], False, True, ["ident", mkey], [tkey],
                           inc=(q == len(grp) - 1))

                def emit_E(i):
                    tile, tkey = sc_tiles[i % 3]
                    pb, pbkey = pt_bufs[i % 2]
                    n = len(groups[i])
                    act(pb[:, 0:n, :], tile[:, 0:n, :], AF.Exp, [tkey], [pbkey], scale=ATT_SCALE)

                def emit_PV(i):
                    pb, pbkey = pt_bufs[i % 2]
                    grp = groups[i]
                    for q, (g, slot, mi) in enumerate(grp):
                        last = (i * 4 + q == nblk - 1)
                        mm(uz[:, 0, :], VR[g][:, slot, s * 128:(s + 1) * 128], pb[:, q, :], False, last,
                           [("VR", g, slot), pbkey], [uzkey], inc=False, zacc=True)
                        mm(uz[:, 1, :], ones[:, :], pb[:, q, :], False, last, ["ones", pbkey], [uzkey],
                           inc=(q == len(grp) - 1), zacc=True)

                emit_S(0)
                for i in range(len(groups)):
                    if i + 1 < len(groups):
                        emit_S(i + 1)
                    emit_E(i)
                    emit_PV(i)
                recip(rZ[:, :], uz[:, 1, :], [uzkey], ["rZ"])
                tt(o_aT[:, s, 0:128], uz[:, 0, :], rZ[:, :], ALU.mult, [uzkey, "rZ"], ["o_aT"])

            for s in range(4):
                blocks = []
                for g, (win, dil) in enumerate(GROUPS):
                    nb = win // 128
                    for kb in range(c - nb, c + 1):
                        if kb < 0:
                            continue
                        if kb == c:
                            mtype = 0
                        elif kb == c - nb:
                            mtype = 2
                        else:
                            mtype = 1
                        mi = {0: (0, None, 1), 1: (2, 3, 4), 2: (5, 6, 7)}[g][mtype]
                        blocks.append((g, kb % RING[g], (mi, kb < npre)))
                nblk = len(blocks)
                groups = [blocks[b0:b0 + 4] for b0 in range(0, nblk, 4)]
                uz, uzkey = uz_tiles[s % 2]
                att_head(s, groups, nblk, uz, uzkey)

            chk(5.5)
            if sample:
                P.emit("dve", lambda e: e.memset(pUZ[:, 2, 0:32], 0.0), writes=["pUZ"])
                for s in range(4):
                    for g in range(3):
                        mm(psc[:4, 0, 0:4], ksT[:, g * 4 + s, 0:4], q_aT[:, g * 4 + s, 128:132], True, True,
                           ["ksT", "q_aT"], ["psc"])
                        act(Ef[:, :], psc[:4, 0, 0:4], AF.Exp, ["psc"], ["Ef"], scale=ATT_SCALE)
                        tt(Dm[:, :], Ef[:, :], oh4[:, :], ALU.mult, ["Ef", "oh4"], ["Dm"])
                        mm(pUZ[:, 2, s * 4:s * 4 + 4], vs_a[:4, g, s * 128:(s + 1) * 128], Dm[:, :], False, False,
                           ["vs_a", "Dm"], ["pUZ"], inc=False, zacc=True)
                        mm(pUZ[:, 2, 16 + s * 4:16 + s * 4 + 4], ones[:4, :], Dm[:, :], False, False,
                           ["ones", "Dm"], ["pUZ"], inc=True, zacc=True)
                for b in range(4):
                    for g, (win, dil) in enumerate(GROUPS):
                        P.dma("pool", "kc", Kc[:, :].rearrange("p (h e) -> p h e", e=128),
                              caches[g][b, 0:win:dil, 0, :, :], writes=["Kc"])
                        P.dma("pool", "vc", Vc[:, :].rearrange("p (h e) -> p h e", e=128),
                              caches[g][b, 0:win:dil, 1, :, :], writes=["Vc"])
                        for s in range(4):
                            tr(pT[:, s, :], Kc[:, s * 128:(s + 1) * 128], ident[:, :], ["Kc", "ident"], ["pT"],
                               inc=(s == 3))
                        act(KTs[:, :, :], pT[:, 0:4, :], AF.Copy, ["pT"], ["KTs"])
                        for s in range(4):
                            mm(psc[:, 1, s:s + 1], KTs[:, s, :], q_aT[:, g * 4 + s, 128 + b:129 + b], True, True,
                               ["KTs", "q_aT"], ["psc"], inc=(s == 3))
                        act(PTs[:, :], psc[:, 1, 0:4], AF.Exp, ["psc"], ["PTs"], scale=ATT_SCALE)
                        for s in range(4):
                            last = (g == 2)
                            mm(pUZ[:, 2, s * 4 + b:s * 4 + b + 1], Vc[:, s * 128:(s + 1) * 128], PTs[:, s:s + 1],
                               False, last, ["Vc", "PTs"], ["pUZ"], inc=False, zacc=True)
                            mm(pUZ[:, 2, 16 + s * 4 + b:16 + s * 4 + b + 1], ones[:, :], PTs[:, s:s + 1],
                               False, last, ["ones", "PTs"], ["pUZ"], inc=True, zacc=True)
                recip(rZ[:, 0:16], pUZ[:, 2, 16:32], ["pUZ"], ["rZ"])
                for s in range(4):
                    tt(o_aT[:, s, 128:132], pUZ[:, 2, s * 4:s * 4 + 4], rZ[:, s * 4:s * 4 + 4], ALU.mult,
                       ["pUZ", "rZ"], ["o_aT"])

            chk(6)
            wb, wkey = w_next()
            for half in range(4):
                ptile, pkey = next_pf()
                for fl in range(2):
                    f = half * 2 + fl
                    for sl in range(4):
                        mm(ptile[:, fl, 0:W], v3(wb, 4, 1024)[:, sl, f * 128:(f + 1) * 128], o_aT[:, sl, 0:W],
                           sl == 0, sl == 3, wkey + ["o_aT"], [pkey])
                for fl in range(2):
                    f = half * 2 + fl
                    tt(tmpf[:, 0:W], ptile[:, fl, 0:W], gb_s[:, f, sc:sc + W], ALU.mult, [pkey, "gb_s"], ["tmpf"])
                    tt(mergedT[:, f, 0:W], tmpf[:, 0:W], mergedT[:, f, 0:W], ALU.add, ["tmpf", ("mergedT", f)],
                       [("mergedT", f)])

            chk(7)
            for u in range(2):
                wb, wkey = w_next()
                for (k, R, col0) in chunks:
                    ptile, pkey = next_pt()
                    for kc in range(8):
                        mm(ptile[:R, :], mergedT[:, kc, col0:col0 + R], v3(wb, 8, 512)[:, kc, :], kc == 0, kc == 7,
                           wkey + [("mergedT", kc)], [pkey])
                    kk = xs_of[k]
                    tt(xh[:R, kk, u * 512:(u + 1) * 512], ptile[:R, :], xh[:R, kk, u * 512:(u + 1) * 512], ALU.add,
                       [pkey, ("xh", kk)], [("xh", kk)])

        def ffn(fchunks):
            chunks = [(k, R, col0) for (k, R, col0, kind, row) in fchunks]
            W = max(col0 + R for (k, R, col0) in chunks)
            chk(8)
            for (k, R, col0) in chunks:
                norm_T(k, R, col0, g2T, "g2T", hnT, "hnT")

            chk(9)
            for u in range(11):
                wb, wkey = w_next()
                for fl in range(2):
                    ptile, pkey = next_pf()
                    for ab in range(2):
                        for kc in range(8):
                            wpart = wb[:, ab * 2048:(ab + 1) * 2048].rearrange("p (a b) -> p a b", b=256)
                            mm(ptile[:, ab, 0:W], wpart[:, kc, fl * 128:(fl + 1) * 128],
                               hnT[:, kc, 0:W], kc == 0, kc == 7, wkey + ["hnT"], [pkey])
                    act(sa_t[:, 0:W], ptile[:, 0, 0:W], AF.Silu, [pkey], ["sa_t"])
                    tt(actT[:, 2 * u + fl, 0:W], ptile[:, 1, 0:W], sa_t[:, 0:W], ALU.mult, [pkey, "sa_t"],
                       [("actT", 2 * u + fl)])

            chk(10)
            for hh in range(2):
                for q in range(4):
                    wb, wkey = w_next()
                    for (k, R, col0) in chunks:
                        ptile, pkey = next_pt()
                        for hl in range(11):
                            hc = hh * 11 + hl
                            mm(ptile[:R, 0:256], actT[:, hc, col0:col0 + R], v3(wb, 11, 256)[:, hl, :],
                               hl == 0, hl == 10, wkey + [("actT", hc)], [pkey])
                        tt(xh[:R, k, q * 256:(q + 1) * 256], ptile[:R, 0:256], xh[:R, k, q * 256:(q + 1) * 256],
                           ALU.add, [pkey, ("xh", k)], [("xh", k)])

            chk(11)
            for (k, R, col0, kind, row) in fchunks:
                act(xn_tmp[:R, :], xh[:R, k, :], AF.Square, [("xh", k)], [("xn_tmp", 0), ("xn_tmp", 1), ("stat", k)],
                    accum=stat[:R, k:k + 1])
                act(stat[:R, 2 + k:3 + k], stat[:R, k:k + 1], AF.Sqrt, [("stat", k), "colt"], [("stat", 2 + k)],
                    scale=1.0 / D, bias=colt[:R, 8:9])
                recip(stat[:R, 4 + k:5 + k], stat[:R, 2 + k:3 + k], [("stat", 2 + k)], [("stat", 4 + k)])
                stt(xh[:R, k, :], xh[:R, k, :], stat[:R, 4 + k:5 + k], lnf_bc[:R, :], ALU.mult, ALU.mult,
                    [("xh", k), ("stat", 4 + k), "lnf_bc"], [("xh", k)])
                if kind == "p":
                    P.dma("sp", "y%d" % k, yp[row * 128:(row + 1) * 128, :], xh[:, k, :], reads=[("xh", k)])
                else:
                    P.dma("sp", "y%d" % k, ys[:, :], xh[:4, k, :], reads=[("xh", k)])

        try:
            for it in plan:
                if it[0] == "body":
                    body(it[1], it[2])
                elif it[0] == "body2":
                    body(it[1], 0, it[2])
                elif it[0] == "shared":
                    body(it[1], 0, it[2], phase="shared")
                elif it[0] == "rest":
                    body(it[1], 0, None, phase="rest", pk0=it[2])
                else:
                    ffn(it[1])
        except StopBuild:
            pass

        for h in (range(4) if stop >= 3 else []):
            P.dma("sp", "sfin", sp_o[h].rearrange("(x p) v -> p x v", p=128), S[:, 2 * h:2 * h + 2, :],
                  reads=[("S", h)])
        P.wait_all_dma("sp")

        with nc.Block() as block:
            P.finish(block)
    return nc


def _tables(hh):
    f32 = np.float32
    half = 128
    inv = (np.float32(10000.0) ** (-np.arange(half, dtype=f32) / f32(half))).astype(f32)
    cs = np.zeros((NCH, 128, 2, WS), f32)
    for c in range(NCH):
        base = c * 128 if hh == 1 else max(c - NPRE, 0) * 128
        pos = np.concatenate([base + np.arange(128), np.full(WS - 128, PAST)]).astype(f32)
        ang = (pos[:, None] * inv[None, :]).astype(f32)
        cs[c, :, 0, :] = np.cos(ang).T
        cs[c, :, 1, :] = np.sin(ang).T
    lg = np.log(1.0 - 2.0 ** (-5.0 - np.arange(4, dtype=np.float64)))
    i = np.arange(128, dtype=np.float64)
    retmask = np.zeros((4, 128, 128), np.float64)
    for h in range(4):
        m = np.exp(-lg[h] * (i[:, None] + 1.0)) / 16.0 * (i[:, None] <= i[None, :])
        retmask[h] = m
    cols = np.zeros((128, 16), np.float64)
    for h in range(4):
        cols[:, h] = np.exp(lg[h] * (127.0 - i)) / 16.0
        cols[:, 4 + h] = EPS * np.exp(-2.0 * lg[h] * (i + 1.0))
    cols[:, 8] = EPS
    for b in range(4):
        cols[b, 9 + b] = 1.0 / 16.0
    jj = np.arange(128)[:, None]
    ii = np.arange(128)[None, :]
    att = np.zeros((8, 128, 128), np.float64)
    k = 0
    for (win, dil) in GROUPS:
        res = ((ii - jj) % dil) == 0
        types = [res & (jj <= ii)]
        if dil > 1:
            types.append(res)
        types.append(res & (jj >= ii))
        for t in types:
            att[k] = np.where(t, 0.0, NEG)
            k += 1
    qmask = np.zeros((128, 16), np.float64)
    for b in range(4):
        qmask[:, 4 * b + b] = 1.0
    oh4 = np.eye(4)
    att_p = att if hh == 1 else np.full_like(att, NEG)
    return dict(cs_tab=cs, retmask=retmask.astype(f32), cols=cols.astype(f32), attmask=att.astype(f32),
                attmask_p=att_p.astype(f32), qmask=qmask.astype(f32), oh4=oh4.astype(f32))


_CACHE = {}


def kernel(x_prompt, x_sample, state_ret, cache_kv_w128, cache_kv_w512, cache_kv_w2048,
           ln1_g, w_in, ret_gn_g, w_pa, w_pb, w_o, ln2_g, w_ffn_in, w_ffn_out, lnf_g):
    f32 = np.float32
    if "nc" not in _CACHE:
        _CACHE["nc"] = build_program()
    nc = _CACHE["nc"]
    tabs = [_tables(0), _tables(1)]
    shared = dict(
        w_in=np.ascontiguousarray(w_in[0], f32), w_pa=np.ascontiguousarray(w_pa[0], f32),
        w_pb=np.ascontiguousarray(w_pb[0], f32), w_o=np.ascontiguousarray(w_o[0], f32),
        w_fi=np.ascontiguousarray(w_ffn_in[0], f32), w_fo=np.ascontiguousarray(w_ffn_out[0], f32),
        ln1=np.ascontiguousarray(ln1_g[0].reshape(8, 128).T, f32),
        ln2=np.ascontiguousarray(ln2_g[0].reshape(8, 128).T, f32),
        lnf=np.ascontiguousarray(np.broadcast_to(lnf_g[None, :], (128, D)), f32),
        gng=np.ascontiguousarray(ret_gn_g[0].reshape(16, 128).T, f32))
    cs_ = [cache_kv_w128, cache_kv_w512, cache_kv_w2048]
    in_maps = []
    for core in range(_CACHE.get("ncores", 8)):
        m = dict(shared)
        b, hh = core // 2, core % 2
        m.update(tabs[hh])
        if hh == 1:
            m["xp"] = np.ascontiguousarray(x_prompt[b], f32)
        else:
            m["xp"] = np.concatenate([np.zeros((NPRE * 128, D), f32), np.asarray(x_prompt[b, :SEQ - NPRE * 128], f32)], 0)
        m["xs"] = np.ascontiguousarray(x_sample[4 * core:4 * core + 4, 0, :], f32)
        m["st0"] = np.ascontiguousarray(state_ret[0, 4 * core:4 * core + 4], f32)
        for g in range(3):
            m["c%d" % g] = np.ascontiguousarray(cs_[g][0, 4 * core:4 * core + 4], f32)
        in_maps.append(m)
    res = run_bass_kernel_spmd(nc, in_maps, core_ids=list(range(len(in_maps))))
    r = res.results
    if len(r) < 8:
        return r
    y_prompt = np.stack([np.concatenate([r[2 * b]["yp"], r[2 * b + 1]["yp"]], 0) for b in range(4)], 0).astype(f32)
    y_sample = np.concatenate([r[k]["ys"] for k in range(8)], 0).reshape(32, 1, D).astype(f32)
    s_p = np.stack([r[2 * b + 1]["sp_o"] for b in range(4)], 0)[None].astype(f32)
    s_s = np.concatenate([r[k]["ss_o"] for k in range(8)], 0)[None].astype(f32)
    outs = [y_prompt, y_sample, s_p, s_s]
    for g in range(3):
        kvp = np.stack([r[2 * b + 1]["kvp%d" % g] for b in range(4)], 0)[None].astype(f32)
        kvs_ = np.concatenate([r[k]["kvs%d" % g] for k in range(8)], 0).reshape(1, 32, 1, 2, 4, 128).astype(f32)
        outs += [kvp, kvs_]
    return tuple(outs)
```

```python
import math
import os
KDBG = os.environ.get('KDBG', '')
from contextlib import ExitStack
import numpy as np
import concourse.bass as bass
import concourse.mybir as mybir
from concourse.bass_utils import run_bass_kernel_spmd

F32 = mybir.dt.float32
BF16 = mybir.dt.bfloat16
AF = mybir.ActivationFunctionType
ALU = mybir.AluOpType

ENGS = ["pe", "act", "dve", "pool", "sp"]

D = 1024
SEQ = 4096
NCH = SEQ // 128
PAST = 16384
EPS = 1e-6
NEG = -30000.0
GROUPS = ((128, 1), (512, 4), (2048, 16))
RING = (2, 5, 17)
FF = 2816
WS = 132
NPRE = 16
ATT_SCALE = 128.0 ** -0.5


class Prog:
    def __init__(self, nc, es):
        self.nc = nc
        self.es = es
        self.ops = {e: [] for e in ENGS}
        self.cnt = {e: 0 for e in ENGS}
        self.known = {e: {} for e in ENGS}
        self.lastw = {}
        self.reads = {}
        self.sems = {e: es.enter_context(nc.semaphore("c_" + e)) for e in ENGS}
        self.dsem = {}
        self.dval = {}
        self.excl = set()

    def _deps(self, eng, reads, writes):
        toks = []
        for r in reads:
            t = self.lastw.get(r)
            if t is not None:
                toks.append(t)
        for w in writes:
            t = self.lastw.get(w)
            if t is not None:
                toks.append(t)
            toks.extend(self.reads.get(w, []))
        waits = {}
        for (kind, key, val) in toks:
            if kind == "eng" and key == eng and eng == "pe":
                continue
            k = (kind, key)
            if self.known[eng].get(k, 0) >= val:
                continue
            if waits.get(k, 0) < val:
                waits[k] = val
        out = []
        for (kind, key), val in waits.items():
            self.known[eng][(kind, key)] = val
            sem = self.sems[key] if kind == "eng" else self.dsem[key]
            out.append((sem, val))
        return out

    def _register(self, tok, reads, writes):
        for r in reads:
            self.reads.setdefault(r, []).append(tok)
        for w in writes:
            self.lastw[w] = tok
            self.reads[w] = []

    def emit(self, eng, fn, reads=(), writes=(), inc=True):
        if self.excl:
            writes = list(writes) + [r for r in reads if r in self.excl]
            reads = [r for r in reads if r not in self.excl]
        waits = self._deps(eng, reads, writes)
        if inc:
            self.cnt[eng] += 1
            tok = ("eng", eng, self.cnt[eng])
        else:
            assert eng == "pe"
            tok = ("eng", eng, self.cnt[eng] + 1)
        self._register(tok, reads, writes)
        sem = self.sems[eng]

        def run(e, waits=waits, fn=fn, sem=sem, inc=inc):
            for (s, v) in waits:
                e.wait_ge(s, v)
            ins = fn(e)
            if inc:
                ins.then_inc(sem, 1)

        self.ops[eng].append(run)

    def dma(self, eng, key, out, in_, reads=(), writes=(), **kw):
        if key not in self.dsem:
            self.dsem[key] = self.es.enter_context(self.nc.semaphore("d_" + str(key)))
            self.dval[key] = 0
        waits = self._deps(eng, reads, writes)
        self.dval[key] += 16
        tok = ("dma", key, self.dval[key])
        self._register(tok, reads, writes)
        sem = self.dsem[key]

        def run(e, waits=waits, sem=sem, out=out, in_=in_, kw=kw):
            for (s, v) in waits:
                e.wait_ge(s, v)
            e.dma_start(out=out, in_=in_, **kw).then_inc(sem, 16)

        self.ops[eng].append(run)

    def wait_all_dma(self, eng):
        ws = []
        for k, v in self.dval.items():
            if self.known[eng].get(("dma", k), 0) < v:
                ws.append((self.dsem[k], v))
                self.known[eng][("dma", k)] = v

        def run(e, ws=ws):
            for (s, v) in ws:
                e.wait_ge(s, v)

        self.ops[eng].append(run)

    def finish(self, block):
        ops = self.ops

        @block.tensor
        def _(e):
            for f in ops["pe"]:
                f(e)

        @block.scalar
        def _(e):
            for f in ops["act"]:
                f(e)

        @block.vector
        def _(e):
            for f in ops["dve"]:
                f(e)

        @block.gpsimd
        def _(e):
            for f in ops["pool"]:
                f(e)

        @block.sync
        def _(e):
            for f in ops["sp"]:
                f(e)


class StopBuild(Exception):
    pass


def build_program(nchunks=NCH, with_sample=True, stop=99, npre=NPRE):
    def chk(n):
        if n > stop:
            raise StopBuild()

    nc = bass.Bass("TRN2", target_bir_lowering=False)

    def din(name, shape):
        return nc.dram_tensor(name, list(shape), F32, kind="ExternalInput").ap()

    def dout(name, shape):
        return nc.dram_tensor(name, list(shape), F32, kind="ExternalOutput").ap()

    xp = din("xp", [SEQ, D])
    xs = din("xs", [4, D])
    st0 = din("st0", [4, 4, 256, 512])
    caches = [din("c%d" % g, [4, GROUPS[g][0], 2, 4, 128]) for g in range(3)]
    w_in = din("w_in", [D, 12800])
    w_pa = din("w_pa", [2048, D])
    w_pb = din("w_pb", [512, D])
    w_o = din("w_o", [D, D])
    w_fi = din("w_fi", [D, 2 * FF])
    w_fo = din("w_fo", [FF, D])
    ln1 = din("ln1", [128, 8])
    ln2 = din("ln2", [128, 8])
    lnf = din("lnf", [128, D])
    gng = din("gng", [128, 16])
    cs_tab = din("cs_tab", [NCH, 128, 2, WS])
    retmask_d = din("retmask", [4, 128, 128])
    attmask_d = din("attmask", [8, 128, 128])
    attmask_pd = din("attmask_p", [8, 128, 128])
    cols_d = din("cols", [128, 16])
    qmask_d = din("qmask", [128, 16])
    oh4_d = din("oh4", [4, 4])

    yp = dout("yp", [SEQ - npre * 128, D])
    ys = dout("ys", [4, D])
    sp_o = dout("sp_o", [4, 256, 512])
    ss_o = dout("ss_o", [4, 4, 256, 512])
    kvp = [dout("kvp%d" % g, [GROUPS[g][0], 2, 4, 128]) for g in range(3)]
    kvs = [dout("kvs%d" % g, [4, 2, 4, 128]) for g in range(3)]

    gam = [1.0 - 2.0 ** (-5.0 - h) for h in range(4)]
    cdec = [g ** 128 for g in gam]

    with ExitStack() as es:
        def sb(name, shape, dt):
            return es.enter_context(nc.sbuf_tensor(name, list(shape), dt))

        def ps(name, shape, dt):
            return es.enter_context(nc.psum_tensor(name, list(shape), dt))

        xh = sb("xh", [128, 2, D], F32)
        xn_tmp = sb("xn_tmp", [128, D], BF16)
        xnT = sb("xnT", [128, 8, 256], BF16)
        hnT = sb("hnT", [128, 8, 256], BF16)
        q_rT = sb("q_rT", [128, 8, 256], BF16)
        k_rT = sb("k_rT", [128, 8, 256], BF16)
        v_r = sb("v_r", [128, 2, 2048], BF16)
        g_s = sb("g_s", [128, 2, 2048], BF16)
        S = sb("S", [128, 8, 512], F32)
        S_bf = sb("S_bf", [128, 8, 512], BF16)
        o_rT = sb("o_rT", [128, 16, WS], BF16)
        ga_s = sb("ga_s", [128, 8, 256], BF16)
        gb_s = sb("gb_s", [128, 8, 256], BF16)
        mergedT = sb("mergedT", [128, 8, WS], BF16)
        q_aT = sb("q_aT", [128, 12, WS], BF16)
        KT = [sb("KT%d" % g, [128, 4, RING[g] * 128], BF16) for g in range(3)]
        VR = [sb("VR%d" % g, [128, RING[g], 512], BF16) for g in range(3)]
        o_aT = sb("o_aT", [128, 4, WS], BF16)
        actT = sb("actT", [128, 22, 256], BF16)
        wbuf = [sb("wbuf%d" % i, [128, 4096], BF16) for i in range(3)]
        cs = sb("cs", [128, 2, 256], F32)
        retmask = sb("retmask_s", [128, 4, 128], F32)
        attmask = sb("attmask_s", [128, 8, 128], BF16)
        attmask_p = sb("attmask_ps", [128, 8, 128], BF16)
        lnf_bc = sb("lnf_bc", [128, D], F32)
        ident = sb("ident", [128, 128], BF16)
        identf = sb("identf", [128, 128], F32)
        ones = sb("ones", [128, 128], BF16)
        g1T = sb("g1T", [128, 8], F32)
        g2T = sb("g2T", [128, 8], F32)
        gnT = sb("gnT", [128, 16], F32)
        colt = sb("colt", [128, 16], F32)
        qmask = sb("qmask_s", [128, 16], BF16)
        oh4 = sb("oh4_s", [4, 4], F32)
        stat = sb("stat", [128, 8], F32)
        rt = [sb("rt%d" % i, [128, 256], F32) for i in range(4)]
        scmT = sb("scmT", [128, 128], BF16)
        kd = sb("kd", [128, 4, 256], BF16)
        o_g = sb("o_g", [128, 512], BF16)
        PT = sb("PT", [128, 4, 128], BF16)
        PT2 = sb("PT2", [128, 4, 128], BF16)
        rZ = sb("rZ", [128, 128], F32)
        kvst = [sb("kvst%d" % i, [128, 512], F32) for i in range(2)]
        tmpf = sb("tmpf", [128, WS], F32)
        sa_t = sb("sa_t", [128, 256], BF16)
        s0t = sb("s0t", [128, 2, 512], F32)
        snew_bf = sb("snew_bf", [128, 2, 512], BF16)
        qm = sb("qm", [128, 2, 4], BF16)
        Kc = sb("Kc", [128, 512], BF16)
        Vc = sb("Vc", [128, 512], BF16)
        KTs = sb("KTs", [128, 4, 128], BF16)
        PTs = sb("PTs", [128, 4], BF16)
        ksT = sb("ksT", [128, 12, 4], BF16)
        vs_a = sb("vs_a", [4, 3, 512], BF16)
        Ef = sb("Ef", [4, 4], F32)
        Dm = sb("Dm", [4, 4], BF16)

        pT = ps("pT", [128, 8, 128], BF16)
        pf = [ps("pf%d" % i, [128, 2, 256], F32) for i in range(2)]
        pt = [ps("pt%d" % i, [128, 512], F32) for i in range(2)]
        psc = ps("psc", [128, 4, 128], F32)
        pS = ps("pS", [128, 512], F32)
        pUZ = ps("pUZ", [128, 4, 128], F32)

        P = Prog(nc, es)
        P.excl.update(["pT", ("pf", 0), ("pf", 1), ("pt", 0), ("pt", 1), "psc", "pS", "pUZ"])

        def mm(out, lhsT, rhs, start, stop, reads, writes, inc=None, zacc=False):
            if inc is None:
                inc = stop
            if zacc:
                P.emit("pe", lambda e: e.matmul(out, lhsT=lhsT, rhs=rhs, start=start, stop=stop,
                                                skip_group_check=True), reads, writes, inc)
            else:
                P.emit("pe", lambda e: e.matmul(out, lhsT=lhsT, rhs=rhs, start=start, stop=stop),
                       reads, writes, inc)

        def tr(out, in_, idn, reads, writes, inc=True):
            P.emit("pe", lambda e: e.transpose(out=out, in_=in_, identity=idn), reads, writes, inc)

        def act(out, in_, func, reads, writes, scale=None, bias=None, accum=None):
            kw = {}
            if scale is not None:
                kw["scale"] = scale
            if bias is not None:
                kw["bias"] = bias
            if accum is not None:
                kw["accum_out"] = accum
            P.emit("act", lambda e: e.activation(out=out, in_=in_, func=func, **kw), reads, writes)

        def tt(out, in0, in1, op, reads, writes, eng="dve"):
            P.emit(eng, lambda e: e.tensor_tensor(out=out, in0=in0, in1=in1, op=op), reads, writes)

        def tsc(out, in0, s1, op0, reads, writes, s2=None, op1=None, eng="dve"):
            if op1 is None:
                P.emit(eng, lambda e: e.tensor_scalar(out=out, in0=in0, scalar1=s1, scalar2=None, op0=op0),
                       reads, writes)
            else:
                P.emit(eng, lambda e: e.tensor_scalar(out=out, in0=in0, scalar1=s1, scalar2=s2, op0=op0, op1=op1),
                       reads, writes)

        def stt(out, in0, scalar, in1, op0, op1, reads, writes):
            P.emit("dve", lambda e: e.scalar_tensor_tensor(out=out, in0=in0, scalar=scalar, in1=in1,
                                                           op0=op0, op1=op1), reads, writes)

        def recip(out, in_, reads, writes):
            P.emit("dve", lambda e: e.reciprocal(out=out, in_=in_), reads, writes)

        def cp(out, in_, reads, writes, eng="dve"):
            P.emit(eng, lambda e: e.tensor_copy(out=out, in_=in_), reads, writes)

        P.dma("sp", "k_ret", retmask[:], retmask_d.rearrange("h j i -> j h i"), writes=["retmask"])
        P.dma("pool", "k_att", attmask[:], attmask_d.rearrange("m j i -> j m i"), writes=["attmask"])
        P.dma("pool", "k_attp", attmask_p[:], attmask_pd.rearrange("m j i -> j m i"), writes=["attmask_p"])
        P.dma("sp", "k_lnf", lnf_bc[:], lnf[:, :], writes=["lnf_bc"])
        P.dma("sp", "k_g1", g1T[:], ln1[:, :], writes=["g1T"])
        P.dma("sp", "k_g2", g2T[:], ln2[:, :], writes=["g2T"])
        P.dma("sp", "k_gn", gnT[:], gng[:, :], writes=["gnT"])
        P.dma("sp", "k_col", colt[:], cols_d[:, :], writes=["colt"])
        P.dma("pool", "k_qm", qmask[:], qmask_d[:, :], writes=["qmask"])
        P.dma("sp", "k_oh", oh4[:], oh4_d[:, :], writes=["oh4"])
        P.emit("pool", lambda e: e.memset(identf[:], 0.0), writes=["identf"])
        P.emit("pool", lambda e: e.affine_select(out=identf[:], in_=identf[:], pattern=[[-1, 128]],
                                                 compare_op=ALU.not_equal, fill=1.0, base=0,
                                                 channel_multiplier=1),
               reads=["identf"], writes=["identf"])
        cp(ident[:], identf[:], ["identf"], ["ident"])
        P.emit("pool", lambda e: e.memset(ones[:], 1.0), writes=["ones"])

        wq = []
        wstate = {"issued": 0, "cur": -1}

        def w_issue_upto(n):
            while wstate["issued"] <= n and wstate["issued"] < len(wq):
                i = wstate["issued"]
                slot = i % 3
                j = wq[i]
                ncols = 2816 if j >= 43 else 4096
                P.dma("sp", "w%d" % slot, wbuf[slot][:, 0:ncols], scr[j][:, 0:ncols],
                      reads=[("scr", j, 0), ("scr", j, 1)], writes=[("wbuf", slot, 0), ("wbuf", slot, 1)])
                wstate["issued"] += 1

        def w_next():
            wstate["cur"] += 1
            n = wstate["cur"]
            w_issue_upto(n + 2)
            return wbuf[n % 3], [("wbuf", n % 3, 0), ("wbuf", n % 3, 1)]

        def v3(buf, a, b):
            return buf[:, 0:a * b].rearrange("p (a b) -> p a b", b=b)

        def unit_win(u):
            return lambda buf: [(v3(buf, 8, 512), w_in[:, 512 * u:512 * (u + 1)].rearrange("(kc p) n -> p kc n", p=128))]

        def unit_wpa(u):
            return lambda buf: [(v3(buf, 16, 256), w_pa[:, 256 * u:256 * (u + 1)].rearrange("(kc p) n -> p kc n", p=128))]

        def unit_wpb():
            return lambda buf: [(v3(buf, 4, 1024), w_pb[:, :].rearrange("(kc p) n -> p kc n", p=128))]

        def unit_wo(u):
            return lambda buf: [(v3(buf, 8, 512), w_o[:, 512 * u:512 * (u + 1)].rearrange("(kc p) n -> p kc n", p=128))]

        def unit_fi(u):
            def f(buf):
                va = buf[:, 0:2048].rearrange("p (a b) -> p a b", b=256)
                vb = buf[:, 2048:4096].rearrange("p (a b) -> p a b", b=256)
                return [(va, w_fi[:, 256 * u:256 * (u + 1)].rearrange("(kc p) n -> p kc n", p=128)),
                        (vb, w_fi[:, FF + 256 * u:FF + 256 * (u + 1)].rearrange("(kc p) n -> p kc n", p=128))]
            return f

        def unit_fo(hh, q):
            return lambda buf: [(v3(buf, 11, 256), w_fo[hh * 1408:(hh + 1) * 1408, 256 * q:256 * (q + 1)].rearrange("(kc p) n -> p kc n", p=128))]

        def iter_has_kout(c, g):
            return c >= NCH - GROUPS[g][0] // 128

        unit_defs = ([unit_win(u) for u in range(25)] + [unit_wpa(u) for u in range(4)] + [unit_wpb()]
                     + [unit_wo(u) for u in range(2)] + [unit_fi(u) for u in range(11)]
                     + [unit_fo(hh, q) for hh in range(2) for q in range(4)])
        NU = len(unit_defs)
        assert NU == 51
        scr = nc.dram_tensor("wscr", [NU, 128, 4096], BF16, kind="Internal").ap()
        for j, ud in enumerate(unit_defs):
            parts = ud(scr[j])
            for pi, (dst, src) in enumerate(parts):
                wr = [("scr", j, 0), ("scr", j, 1)] if len(parts) == 1 else [("scr", j, pi)]
                P.dma("pool", "cv%d" % j, dst, src, writes=wr)
        SHARED_UNITS = [0, 1, 2, 3, 4, 5, 6, 7, 8, 9, 10, 11, 21, 22, 23, 24]

        def units_for(c):
            if c >= npre:
                return list(range(32))
            need = [g for g in range(3) if c >= npre - GROUPS[g][0] // 128]
            return [2, 3, 4, 5, 6, 7] + [15 + g for g in need] + [18 + g for g in need]

        plan = []
        c_ = 0
        while c_ < nchunks:
            if c_ + 1 < npre and c_ + 1 < nchunks:
                plan.append(("body2", c_, c_ + 1))
                c_ += 2
            elif c_ < npre:
                plan.append(("body", c_, 0))
                c_ += 1
            elif c_ == npre or c_ == nchunks - 1:
                plan.append(("body", c_, 0))
                fl_ = [(0, 128, 0, "p", c_ - npre)]
                if with_sample and c_ == npre:
                    fl_.append((1, 4, 128, "s", 0))
                plan.append(("ffn", fl_))
                c_ += 1
            else:
                plan.append(("shared", c_, c_ + 1))
                plan.append(("rest", c_, 0))
                plan.append(("rest", c_ + 1, 1))
                plan.append(("ffn", [(0, 128, 0, "p", c_ - npre), (1, 128, 128, "p", c_ + 1 - npre)]))
                c_ += 2
        for it in plan:
            if it[0] == "body":
                wq.extend(units_for(it[1]))
            elif it[0] == "body2":
                wq.extend(units_for(it[2]))
            elif it[0] == "shared":
                wq.extend(SHARED_UNITS)
            elif it[0] == "rest":
                wq.extend([u_ for u_ in range(32) if u_ not in SHARED_UNITS])
            else:
                wq.extend(range(32, NU))

        def norm_T(k, R, col0, gT, gkey, dstT, dkey):
            act(xn_tmp[:R, :], xh[:R, k, :], AF.Square, [("xh", k)], [("xn_tmp", 0), ("xn_tmp", 1), ("stat", k)],
                accum=stat[:R, k:k + 1])
            act(stat[:R, 2 + k:3 + k], stat[:R, k:k + 1], AF.Sqrt, [("stat", k), "colt"], [("stat", 2 + k)],
                scale=1.0 / D, bias=colt[:R, 8:9])
            recip(stat[:R, 4 + k:5 + k], stat[:R, 2 + k:3 + k], [("stat", 2 + k)], [("stat", 4 + k)])
            for hf in range(2):
                tsc(xn_tmp[:R, hf * 512:(hf + 1) * 512], xh[:R, k, hf * 512:(hf + 1) * 512], stat[:R, 4 + k:5 + k],
                    ALU.mult, [("xh", k), ("stat", 4 + k)], [("xn_tmp", hf)])
            for kc in range(8):
                tr(pT[:, kc, 0:R], xn_tmp[:R, kc * 128:(kc + 1) * 128], ident[:R, :R],
                   [("xn_tmp", kc // 4), "ident"], ["pT"], inc=(kc == 7))
            for kc in range(8):
                if kc % 2 == 0:
                    act(dstT[:, kc, col0:col0 + R], pT[:, kc, 0:R], AF.Copy, ["pT", gkey], [dkey],
                        scale=gT[:, kc:kc + 1])
                else:
                    tsc(dstT[:, kc, col0:col0 + R], pT[:, kc, 0:R], gT[:, kc:kc + 1], ALU.mult, ["pT", gkey], [dkey])

        pfi = [0]
        pti = [0]

        def next_pf():
            pfi[0] ^= 1
            return pf[pfi[0]], ("pf", pfi[0])

        def next_pt():
            pti[0] ^= 1
            return pt[pti[0]], ("pt", pti[0])

        kvi = [0]

        def next_kvst():
            kvi[0] ^= 1
            return kvst[kvi[0]], ("kvst", kvi[0])

        def body(c, kx, c2=None, phase=None, pk0=0):
            prefix = c < npre
            pair = c2 is not None
            sample = with_sample and c == npre and phase is None
            ulist = units_for(c2 if pair else c)
            if phase == "shared":
                ulist = SHARED_UNITS
            elif phase == "rest":
                ulist = [u_ for u_ in range(32) if u_ not in SHARED_UNITS]
            W = 256 if pair else (WS if sample else 128)
            chunks = [(0, 128, 0)] + ([(1, 4, 128)] if sample else [])
            if pair:
                chunks = [(0, 128, 0), (1, 128, 128)]
            if phase == "rest":
                chunks = [(pk0, 128, 0)]
            sc = 128 * pk0 if phase == "rest" else 0
            pchunks = [t for t in chunks if t[1] == 128]
            cids = {0: c, 1: c2}
            if phase == "rest":
                cids = {pk0: c}
            xs_of = {0: kx, 1: 1}
            cskeys = [("cs", 0), ("cs", 1)]

            if phase != "rest":
                P.dma("pool", "x%d" % kx, xh[:, kx, :], xp[c * 128:(c + 1) * 128, :], writes=[("xh", kx)])
            if phase == "rest":
                pass
            elif pair:
                P.dma("pool", "x1", xh[:, 1, :], xp[c2 * 128:(c2 + 1) * 128, :], writes=[("xh", 1)])
                P.dma("pool", "cs0", cs[:, :, 0:128], cs_tab[c][:, :, 0:128], writes=[("cs", 0)])
                P.dma("pool", "cs1", cs[:, :, 128:256], cs_tab[c2][:, :, 0:128], writes=[("cs", 1)])
            else:
                P.dma("pool", "cs0", cs[:, :, 0:W], cs_tab[c][:, :, 0:W], writes=cskeys)
            if sample:
                P.dma("pool", "x1", xh[:4, 1, :], xs[:, :], writes=[("xh", 1)])
            chk(1)
            for (k, R, col0) in (chunks if phase != "rest" else []):
                norm_T(xs_of[k], R, col0, g1T, "g1T", xnT, "xnT")

            chk(2)
            def feat_unit(wb, wkey, nf, evac):
                for half in range(nf // 2):
                    ptile, pkey = next_pf()
                    for fl in range(2):
                        f = half * 2 + fl
                        for kc in range(8):
                            if 'nomm' in KDBG:
                                continue
                            mm(ptile[:, fl, 0:W], v3(wb, 8, 512)[:, kc, f * 128:(f + 1) * 128], xnT[:, kc, sc:sc + W],
                               kc == 0, kc == 7, wkey + ["xnT"], [pkey])
                    if 'noevac' not in KDBG:
                        evac(half, ptile, pkey)

            def tok_unit(wb, wkey, evac):
                for (k, R, col0) in chunks:
                    ptile, pkey = next_pt()
                    for kc in range(8):
                        mm(ptile[:R, :], xnT[:, kc, sc + col0:sc + col0 + R], v3(wb, 8, 512)[:, kc, :],
                           kc == 0, kc == 7, wkey + ["xnT"], [pkey])
                    evac(k, R, ptile, pkey)

            for u in range(25):
                chk(2 + (u + 1) / 100.0)
                if u not in ulist:
                    continue
                wb, wkey = w_next()
                if u < 4:
                    dstT, dkey = (q_rT, "q_rT") if u < 2 else (k_rT, "k_rT")
                    hbase = (u % 2) * 2

                    def evac_rot(half, ptile, pkey, dstT=dstT, dkey=dkey, hbase=hbase):
                        h = hbase + half
                        A = ptile[:, 0, 0:W]
                        B = ptile[:, 1, 0:W]
                        cosv = cs[:, 0, 0:W]
                        sinv = cs[:, 1, 0:W]
                        tt(rt[0][:, 0:W], A, cosv, ALU.mult, [pkey] + cskeys, [("rt", 0)])
                        tt(rt[1][:, 0:W], B, sinv, ALU.mult, [pkey] + cskeys, [("rt", 1)])
                        tt(rt[2][:, 0:W], A, sinv, ALU.mult, [pkey] + cskeys, [("rt", 2)])
                        tt(rt[3][:, 0:W], B, cosv, ALU.mult, [pkey] + cskeys, [("rt", 3)])
                        tt(dstT[:, 2 * h, 0:W], rt[0][:, 0:W], rt[1][:, 0:W], ALU.subtract,
                           [("rt", 0), ("rt", 1)], [dkey])
                        tt(dstT[:, 2 * h + 1, 0:W], rt[2][:, 0:W], rt[3][:, 0:W], ALU.add,
                           [("rt", 2), ("rt", 3)], [dkey])
                    feat_unit(wb, wkey, 4, evac_rot)
                elif u < 12:
                    isv = u < 8
                    ucol = (u - 4) % 4

                    def evac_vg(k, R, ptile, pkey, isv=isv, ucol=ucol):
                        if isv:
                            act(v_r[:R, k, ucol * 512:(ucol + 1) * 512], ptile[:R, :], AF.Copy, [pkey], [("v_r", k)])
                        else:
                            act(g_s[:R, k, ucol * 512:(ucol + 1) * 512], ptile[:R, :], AF.Silu, [pkey], [("g_s", k)])
                    tok_unit(wb, wkey, evac_vg)
                elif u < 15:
                    g = u - 12

                    def evac_qa(half, ptile, pkey, g=g):
                        for fl in range(2):
                            act(q_aT[:, g * 4 + half * 2 + fl, 0:W], ptile[:, fl, 0:W], AF.Copy, [pkey], ["q_aT"])
                    feat_unit(wb, wkey, 4, evac_qa)
                elif u < 18:
                    g = u - 15

                    def evac_ka(half, ptile, pkey, g=g):
                        for fl in range(2):
                            h = half * 2 + fl
                            for (pk, pR, pcol) in pchunks:
                                slot = cids[pk] % RING[g]
                                act(KT[g][:, h, slot * 128:(slot + 1) * 128], ptile[:, fl, pcol:pcol + 128], AF.Copy,
                                    [pkey], [("KT", g, slot)])
                            if sample:
                                act(ksT[:, g * 4 + h, 0:4], ptile[:, fl, 128:132], AF.Copy, [pkey], ["ksT"])
                    feat_unit(wb, wkey, 4, evac_ka)
                    if iter_has_kout(c, g) or sample:
                        def evac_kout(k, R, ptile, pkey, g=g):
                            if R == 128 and not iter_has_kout(cids[k], g):
                                return
                            st, skey = next_kvst()
                            cp(st[:R, :], ptile[:R, :], [pkey], [skey])
                            if R == 128:
                                r0 = cids[k] * 128 - (SEQ - GROUPS[g][0])
                                P.dma("sp", "st_" + str(skey[1]), kvp[g][r0:r0 + 128, 0, :, :],
                                      st[:, :].rearrange("p (h e) -> p h e", e=128), reads=[skey])
                            else:
                                P.dma("sp", "st_" + str(skey[1]), kvs[g][:, 0, :, :],
                                      st[:4, :].rearrange("p (h e) -> p h e", e=128), reads=[skey])
                        tok_unit(wb, wkey, evac_kout)
                elif u < 21:
                    g = u - 18

                    def evac_va(k, R, ptile, pkey, g=g):
                        if R == 128:
                            slot = cids[k] % RING[g]
                            act(VR[g][:, slot, :], ptile[:, :], AF.Copy, [pkey], [("VR", g, slot)])
                        elif 'nova1' not in KDBG:
                            act(vs_a[:4, g, :], ptile[:4, :], AF.Copy, [pkey], ["vs_a"])
                        if ((R == 128 and iter_has_kout(cids[k], g)) or R == 4) and 'nova2' not in KDBG:
                            st, skey = next_kvst()
                            cp(st[:R, :], ptile[:R, :], [pkey], [skey])
                            if R == 128:
                                r0 = cids[k] * 128 - (SEQ - GROUPS[g][0])
                                P.dma("sp", "st_" + str(skey[1]), kvp[g][r0:r0 + 128, 1, :, :],
                                      st[:, :].rearrange("p (h e) -> p h e", e=128), reads=[skey])
                            else:
                                P.dma("sp", "st_" + str(skey[1]), kvs[g][:, 1, :, :],
                                      st[:4, :].rearrange("p (h e) -> p h e", e=128), reads=[skey])
                    tok_unit(wb, wkey, evac_va)
                else:
                    dst, dkey = (ga_s, "ga_s") if u < 23 else (gb_s, "gb_s")
                    fbase = ((u - 21) % 2) * 4

                    def evac_gate(half, ptile, pkey, dst=dst, dkey=dkey, fbase=fbase):
                        for fl in range(2):
                            act(dst[:, fbase + half * 2 + fl, 0:W], ptile[:, fl, 0:W], AF.Sigmoid, [pkey], [dkey])
                    feat_unit(wb, wkey, 4, evac_gate)

            if phase == "shared":
                return
            chk(3)
            def gn_and_T(po, pokey, R, k, col0, h, epsc):
                gn_B(po, pokey, R, k, col0, h, epsc)
                gn_C(R, col0, h)

            def gn_B(po, pokey, R, k, col0, h, epsc):
                act(xn_tmp[:R, 0:512], po[:R, :], AF.Square, [pokey], [("xn_tmp", 0), ("stat", 6)], accum=stat[:R, 6:7])
                act(stat[:R, 7:8], stat[:R, 6:7], AF.Sqrt, [("stat", 6), "colt"], [("stat", 7)],
                    scale=1.0 / 512.0, bias=epsc)
                recip(stat[:R, 6:7], stat[:R, 7:8], [("stat", 7)], [("stat", 6)])
                stt(o_g[:R, :], po[:R, :], stat[:R, 6:7], g_s[:R, k, h * 512:(h + 1) * 512], ALU.mult, ALU.mult,
                    [pokey, ("stat", 6), ("g_s", k)], ["o_g"])

            pT2 = pf[1][:, :, :].bitcast(BF16)[:, 0, :].rearrange("p (a b) -> p a b", b=128)
            pT2key = ("pf", 1)

            def gn_C(R, col0, h):
                for vq in range(4):
                    tr(pT2[:, vq, 0:R], o_g[:R, vq * 128:(vq + 1) * 128], ident[:R, :R], ["o_g", "ident"], [pT2key],
                       inc=(vq == 3))
                for vq in range(4):
                    if vq % 2 == 0:
                        act(o_rT[:, h * 4 + vq, col0:col0 + R], pT2[:, vq, 0:R], AF.Copy, [pT2key, "gnT"], ["o_rT"],
                            scale=gnT[:, h * 4 + vq:h * 4 + vq + 1])
                    else:
                        tsc(o_rT[:, h * 4 + vq, col0:col0 + R], pT2[:, vq, 0:R], gnT[:, h * 4 + vq:h * 4 + vq + 1],
                            ALU.mult, [pT2key, "gnT"], ["o_rT"])

            for (pk, pR, pcol) in pchunks:
              cid = cids[pk]
              for h in range(4):
                for X in range(2):
                    tr(pT[:, X, :], k_rT[:, 2 * h + X, sc + pcol:sc + pcol + 128], ident[:, :], ["k_rT", "ident"], ["pT"], inc=(X == 1))
                act(kd[:, h, :], pT[:, 0:2, :].rearrange("p a b -> p (a b)"), AF.Copy, ["pT", "colt"], [("kd", h)],
                    scale=colt[:, h:h + 1])
                if not prefix:
                    for X in range(2):
                        mm(psc[:, 0, :], k_rT[:, 2 * h + X, sc + pcol:sc + pcol + 128], q_rT[:, 2 * h + X, sc:sc + 128], X == 0, X == 1,
                           ["k_rT", "q_rT"], ["psc"])
                    tt(scmT[:, :], psc[:, 0, :], retmask[:, h, :], ALU.mult, ["psc", "retmask"], ["scmT"])
                for X in range(2):
                    pb, pbkey = [(pS, "pS"), (pf[0][:, :, :].rearrange("p a b -> p (a b)"), ("pf", 0))][X]
                    mm(pb[:, :], kd[:, h, X * 128:(X + 1) * 128], v_r[:, pk, h * 512:(h + 1) * 512], True, True,
                       [("kd", h), ("v_r", pk)], [pbkey])
                    if cid == 0:
                        cp(S[:, 2 * h + X, :], pb[:, :], [pbkey], [("S", h)])
                    else:
                        stt(S[:, 2 * h + X, :], S[:, 2 * h + X, :], float(cdec[h]), pb[:, :], ALU.mult, ALU.add,
                            [pbkey, ("S", h)], [("S", h)])
                if not prefix:
                    po, pokey = next_pt()
                    last_is_intra = (cid == 0)
                    mm(po[:, :], scmT[:, :], v_r[:, pk, h * 512:(h + 1) * 512], True, last_is_intra,
                       ["scmT", ("v_r", pk)], [pokey])
                    if cid > 0:
                        for X in range(2):
                            mm(po[:, :], q_rT[:, 2 * h + X, sc:sc + 128], S_bf[:, 2 * h + X, :], False, X == 1,
                               ["q_rT", ("S_bf", h)], [pokey])
                if npre - 1 <= cid < nchunks - 1:
                    for X in range(2):
                        act(S_bf[:, 2 * h + X, :], S[:, 2 * h + X, :], AF.Copy, [("S", h)], [("S_bf", h)])
                if not prefix:
                    if h > 0:
                        gn_C(128, 0, h - 1)
                    gn_B(po, pokey, 128, pk, 0, h, colt[:, 4 + h:5 + h])
                    if h == 3:
                        gn_C(128, 0, 3)

            if prefix:
                return
            chk(3.5)
            if sample:
                for h in range(4):
                    for X in range(2):
                        tr(pT[:4, X, :], k_rT[:, 2 * h + X, 128:132], ident[:, :], ["k_rT", "ident"], ["pT"],
                           inc=(X == 1))
                    for b in range(4):
                        act(kd[:4, b, :], pT[:4, 0:2, :].rearrange("p a b -> p (a b)"), AF.Copy, ["pT", "colt"],
                            [("kd", b)], scale=colt[:4, 9 + b:10 + b])
                    po, pokey = next_pt()
                    for b in range(4):
                        for X in range(2):
                            P.dma("sp", "s0_%d" % X, s0t[:, X, :], st0[b, h, X * 128:(X + 1) * 128, :],
                                  writes=[("s0t", X)])
                        for X in range(2):
                            tt(qm[:, X, :], q_rT[:, 2 * h + X, 128:132], qmask[:, 4 * b:4 * b + 4], ALU.mult,
                               ["q_rT", "qmask"], [("qm", X)])
                        for X in range(2):
                            pb, pbkey = [(pS, "pS"), (pf[0][:, :, :].rearrange("p a b -> p (a b)"), ("pf", 0))][X]
                            mm(pb[:, :], kd[:4, b, X * 128:(X + 1) * 128], v_r[:4, 1, h * 512:(h + 1) * 512],
                               True, True, [("kd", b), ("v_r", 1)], [pbkey])
                            stt(s0t[:, X, :], s0t[:, X, :], float(gam[h]), pb[:, :], ALU.mult, ALU.add,
                                [pbkey, ("s0t", X)], [("s0t", X)])
                            act(snew_bf[:, X, :], s0t[:, X, :], AF.Copy, [("s0t", X)], [("snew_bf", X)])
                            P.dma("sp", "s0o_%d" % X, ss_o[b, h, X * 128:(X + 1) * 128, :], s0t[:, X, :],
                                  reads=[("s0t", X)])
                        for X in range(2):
                            mm(po[:4, :], qm[:, X, :], snew_bf[:, X, :], b == 0 and X == 0, b == 3 and X == 1,
                               [("qm", X), ("snew_bf", X)], [pokey])
                    gn_and_T(po, pokey, 4, 1, 128, h, colt[:4, 8:9])

            chk(4)
            for u in range(4):
                wb, wkey = w_next()
                ptile, pkey = next_pf()
                for fl in range(2):
                    for vc in range(16):
                        mm(ptile[:, fl, 0:W], v3(wb, 16, 256)[:, vc, fl * 128:(fl + 1) * 128], o_rT[:, vc, 0:W],
                           vc == 0, vc == 15, wkey + ["o_rT"], [pkey])
                for fl in range(2):
                    f = u * 2 + fl
                    tt(mergedT[:, f, 0:W], ptile[:, fl, 0:W], ga_s[:, f, sc:sc + W], ALU.mult, [pkey, "ga_s"],
                       [("mergedT", f)])

            chk(5)
            sc_tiles = [(psc, "psc"),
                        (pf[0][:, :, :].rearrange("p a (b c) -> p (a b) c", c=128), ("pf", 0)),
                        (pf[1][:, :, :].rearrange("p a (b c) -> p (a b) c", c=128), ("pf", 1))]
            pt_bufs = [(PT, "PT"), (PT2, "PT2")]
            uz_tiles = [(pUZ, "pUZ"), (pS[:, :].rearrange("p (a b) -> p a b", b=128), "pS")]

            def att_head(s, groups, nblk, uz, uzkey):
                def memz(e, uz=uz):
                    return e.memset(uz[:, 0:2, :], 0.0)
                P.emit("dve", memz, writes=[uzkey])

                def emit_S(i):
                    tile, tkey = sc_tiles[i % 3]
                    grp = groups[i]
                    for q, (g, slot, mi) in enumerate(grp):
                        mm(tile[:, q, :], KT[g][:, s, slot * 128:(slot + 1) * 128], q_aT[:, g * 4 + s, 0:128],
                           True, False, [("KT", g, slot), "q_aT"], [tkey], inc=False)
                        mtile, mkey = (attmask_p, "attmask_p") if mi[1] else (attmask, "attmask")
                        mm(tile[:, q, :], ident[:, :], mtile[:, mi[0], :], False, True, ["ident", mkey], [tkey],
                           inc=(q == len(grp) - 1))

                def emit_E(i):
                    tile, tkey = sc_tiles[i % 3]
                    pb, pbkey = pt_bufs[i % 2]
                    n = len(groups[i])
                    act(pb[:, 0:n, :], tile[:, 0:n, :], AF.Exp, [tkey], [pbkey], scale=ATT_SCALE)

                def emit_PV(i):
                    pb, pbkey = pt_bufs[i % 2]
                    grp = groups[i]
                    for q, (g, slot, mi) in enumerate(grp):
                        last = (i * 4 + q == nblk - 1)
                        mm(uz[:, 0, :], VR[g][:, slot, s * 128:(s + 1) * 128], pb[:, q, :], False, last,
                           [("VR", g, slot), pbkey], [uzkey], inc=False, zacc=True)
                        mm(uz[:, 1, :], ones[:, :], pb[:, q, :], False, last, ["ones", pbkey], [uzkey],
                           inc=(q == len(grp) - 1), zacc=True)

                emit_S(0)
                for i in range(len(groups)):
                    if i + 1 < len(groups):
                        emit_S(i + 1)
                    emit_E(i)
                    emit_PV(i)
                recip(rZ[:, :], uz[:, 1, :], [uzkey], ["rZ"])
                tt(o_aT[:, s, 0:128], uz[:, 0, :], rZ[:, :], ALU.mult, [uzkey, "rZ"], ["o_aT"])

            for s in range(4):
                blocks = []
                for g, (win, dil) in enumerate(GROUPS):
                    nb = win // 128
                    for kb in range(c - nb, c + 1):
                        if kb < 0:
                            continue
                        if kb == c:
                            mtype = 0
                        elif kb == c - nb:
                            mtype = 2
                        else:
                            mtype = 1
                        mi = {0: (0, None, 1), 1: (2, 3, 4), 2: (5, 6, 7)}[g][mtype]
                        blocks.append((g, kb % RING[g], (mi, kb < npre)))
                nblk = len(blocks)
                groups = [blocks[b0:b0 + 4] for b0 in range(0, nblk, 4)]
                uz, uzkey = uz_tiles[s % 2]
                att_head(s, groups, nblk, uz, uzkey)

            chk(5.5)
            if sample:
                P.emit("dve", lambda e: e.memset(pUZ[:, 2, 0:32], 0.0), writes=["pUZ"])
                for s in range(4):
                    for g in range(3):
                        mm(psc[:4, 0, 0:4], ksT[:, g * 4 + s, 0:4], q_aT[:, g * 4 + s, 128:132], True, True,
                           ["ksT", "q_aT"], ["psc"])
                        act(Ef[:, :], psc[:4, 0, 0:4], AF.Exp, ["psc"], ["Ef"], scale=ATT_SCALE)
                        tt(Dm[:, :], Ef[:, :], oh4[:, :], ALU.mult, ["Ef", "oh4"], ["Dm"])
                        mm(pUZ[:, 2, s * 4:s * 4 + 4], vs_a[:4, g, s * 128:(s + 1) * 128], Dm[:, :], False, False,
                           ["vs_a", "Dm"], ["pUZ"], inc=False, zacc=True)
                        mm(pUZ[:, 2, 16 + s * 4:16 + s * 4 + 4], ones[:4, :], Dm[:, :], False, False,
                           ["ones", "Dm"], ["pUZ"], inc=True, zacc=True)
                for b in range(4):
                    for g, (win, dil) in enumerate(GROUPS):
                        P.dma("pool", "kc", Kc[:, :].rearrange("p (h e) -> p h e", e=128),
                              caches[g][b, 0:win:dil, 0, :, :], writes=["Kc"])
                        P.dma("pool", "vc", Vc[:, :].rearrange("p (h e) -> p h e", e=128),
                              caches[g][b, 0:win:dil, 1, :, :], writes=["Vc"])
                        for s in range(4):
                            tr(pT[:, s, :], Kc[:, s * 128:(s + 1) * 128], ident[:, :], ["Kc", "ident"], ["pT"],
                               inc=(s == 3))
                        act(KTs[:, :, :], pT[:, 0:4, :], AF.Copy, ["pT"], ["KTs"])
                        for s in range(4):
                            mm(psc[:, 1, s:s + 1], KTs[:, s, :], q_aT[:, g * 4 + s, 128 + b:129 + b], True, True,
                               ["KTs", "q_aT"], ["psc"], inc=(s == 3))
                        act(PTs[:, :], psc[:, 1, 0:4], AF.Exp, ["psc"], ["PTs"], scale=ATT_SCALE)
                        for s in range(4):
                            last = (g == 2)
                            mm(pUZ[:, 2, s * 4 + b:s * 4 + b + 1], Vc[:, s * 128:(s + 1) * 128], PTs[:, s:s + 1],
                               False, last, ["Vc", "PTs"], ["pUZ"], inc=False, zacc=True)
                            mm(pUZ[:, 2, 16 + s * 4 + b:16 + s * 4 + b + 1], ones[:, :], PTs[:, s:s + 1],
                               False, last, ["ones", "PTs"], ["pUZ"], inc=True, zacc=True)
                recip(rZ[:, 0:16], pUZ[:, 2, 16:32], ["pUZ"], ["rZ"])
                for s in range(4):
                    tt(o_aT[:, s, 128:132], pUZ[:, 2, s * 4:s * 4 + 4], rZ[:, s * 4:s * 4 + 4], ALU.mult,
                       ["pUZ", "rZ"], ["o_aT"])

            chk(6)
            wb, wkey = w_next()
            for half in range(4):
                ptile, pkey = next_pf()
                for fl in range(2):
                    f = half * 2 + fl
                    for sl in range(4):
                        mm(ptile[:, fl, 0:W], v3(wb, 4, 1024)[:, sl, f * 128:(f + 1) * 128], o_aT[:, sl, 0:W],
                           sl == 0, sl == 3, wkey + ["o_aT"], [pkey])
                for fl in range(2):
                    f = half * 2 + fl
                    tt(tmpf[:, 0:W], ptile[:, fl, 0:W], gb_s[:, f, sc:sc + W], ALU.mult, [pkey, "gb_s"], ["tmpf"])
                    tt(mergedT[:, f, 0:W], tmpf[:, 0:W], mergedT[:, f, 0:W], ALU.add, ["tmpf", ("mergedT", f)],
                       [("mergedT", f)])

            chk(7)
            for u in range(2):
                wb, wkey = w_next()
                for (k, R, col0) in chunks:
                    ptile, pkey = next_pt()
                    for kc in range(8):
                        mm(ptile[:R, :], mergedT[:, kc, col0:col0 + R], v3(wb, 8, 512)[:, kc, :], kc == 0, kc == 7,
                           wkey + [("mergedT", kc)], [pkey])
                    kk = xs_of[k]
                    tt(xh[:R, kk, u * 512:(u + 1) * 512], ptile[:R, :], xh[:R, kk, u * 512:(u + 1) * 512], ALU.add,
                       [pkey, ("xh", kk)], [("xh", kk)])

        def ffn(fchunks):
            chunks = [(k, R, col0) for (k, R, col0, kind, row) in fchunks]
            W = max(col0 + R for (k, R, col0) in chunks)
            chk(8)
            for (k, R, col0) in chunks:
                norm_T(k, R, col0, g2T, "g2T", hnT, "hnT")

            chk(9)
            for u in range(11):
                wb, wkey = w_next()
                for fl in range(2):
                    ptile, pkey = next_pf()
                    for ab in range(2):
                        for kc in range(8):
                            wpart = wb[:, ab * 2048:(ab + 1) * 2048].rearrange("p (a b) -> p a b", b=256)
                            mm(ptile[:, ab, 0:W], wpart[:, kc, fl * 128:(fl + 1) * 128],
                               hnT[:, kc, 0:W], kc == 0, kc == 7, wkey + ["hnT"], [pkey])
                    act(sa_t[:, 0:W], ptile[:, 0, 0:W], AF.Silu, [pkey], ["sa_t"])
                    tt(actT[:, 2 * u + fl, 0:W], ptile[:, 1, 0:W], sa_t[:, 0:W], ALU.mult, [pkey, "sa_t"],
                       [("actT", 2 * u + fl)])

            chk(10)
            for hh in range(2):
                for q in range(4):
                    wb, wkey = w_next()
                    for (k, R, col0) in chunks:
                        ptile, pkey = next_pt()
                        for hl in range(11):
                            hc = hh * 11 + hl
                            mm(ptile[:R, 0:256], actT[:, hc, col0:col0 + R], v3(wb, 11, 256)[:, hl, :],
                               hl == 0, hl == 10, wkey + [("actT", hc)], [pkey])
                        tt(xh[:R, k, q * 256:(q + 1) * 256], ptile[:R, 0:256], xh[:R, k, q * 256:(q + 1) * 256],
                           ALU.add, [pkey, ("xh", k)], [("xh", k)])

            chk(11)
            for (k, R, col0, kind, row) in fchunks:
                act(xn_tmp[:R, :], xh[:R, k, :], AF.Square, [("xh", k)], [("xn_tmp", 0), ("xn_tmp", 1), ("stat", k)],
                    accum=stat[:R, k:k + 1])
                act(stat[:R, 2 + k:3 + k], stat[:R, k:k + 1], AF.Sqrt, [("stat", k), "colt"], [("stat", 2 + k)],
                    scale=1.0 / D, bias=colt[:R, 8:9])
                recip(stat[:R, 4 + k:5 + k], stat[:R, 2 + k:3 + k], [("stat", 2 + k)], [("stat", 4 + k)])
                stt(xh[:R, k, :], xh[:R, k, :], stat[:R, 4 + k:5 + k], lnf_bc[:R, :], ALU.mult, ALU.mult,
                    [("xh", k), ("stat", 4 + k), "lnf_bc"], [("xh", k)])
                if kind == "p":
                    P.dma("sp", "y%d" % k, yp[row * 128:(row + 1) * 128, :], xh[:, k, :], reads=[("xh", k)])
                else:
                    P.dma("sp", "y%d" % k, ys[:, :], xh[:4, k, :], reads=[("xh", k)])

        try:
            for it in plan:
                if it[0] == "body":
                    body(it[1], it[2])
                elif it[0] == "body2":
                    body(it[1], 0, it[2])
                elif it[0] == "shared":
                    body(it[1], 0, it[2], phase="shared")
                elif it[0] == "rest":
                    body(it[1], 0, None, phase="rest", pk0=it[2])
                else:
                    ffn(it[1])
        except StopBuild:
            pass

        for h in (range(4) if stop >= 3 else []):
            P.dma("sp", "sfin", sp_o[h].rearrange("(x p) v -> p x v", p=128), S[:, 2 * h:2 * h + 2, :],
                  reads=[("S", h)])
        P.wait_all_dma("sp")

        with nc.Block() as block:
            P.finish(block)
    return nc


def _tables(hh):
    f32 = np.float32
    half = 128
    inv = (np.float32(10000.0) ** (-np.arange(half, dtype=f32) / f32(half))).astype(f32)
    cs = np.zeros((NCH, 128, 2, WS), f32)
    for c in range(NCH):
        base = c * 128 if hh == 1 else max(c - NPRE, 0) * 128
        pos = np.concatenate([base + np.arange(128), np.full(WS - 128, PAST)]).astype(f32)
        ang = (pos[:, None] * inv[None, :]).astype(f32)
        cs[c, :, 0, :] = np.cos(ang).T
        cs[c, :, 1, :] = np.sin(ang).T
    lg = np.log(1.0 - 2.0 ** (-5.0 - np.arange(4, dtype=np.float64)))
    i = np.arange(128, dtype=np.float64)
    retmask = np.zeros((4, 128, 128), np.float64)
    for h in range(4):
        m = np.exp(-lg[h] * (i[:, None] + 1.0)) / 16.0 * (i[:, None] <= i[None, :])
        retmask[h] = m
    cols = np.zeros((128, 16), np.float64)
    for h in range(4):
        cols[:, h] = np.exp(lg[h] * (127.0 - i)) / 16.0
        cols[:, 4 + h] = EPS * np.exp(-2.0 * lg[h] * (i + 1.0))
    cols[:, 8] = EPS
    for b in range(4):
        cols[b, 9 + b] = 1.0 / 16.0
    jj = np.arange(128)[:, None]
    ii = np.arange(128)[None, :]
    att = np.zeros((8, 128, 128), np.float64)
    k = 0
    for (win, dil) in GROUPS:
        res = ((ii - jj) % dil) == 0
        types = [res & (jj <= ii)]
        if dil > 1:
            types.append(res)
        types.append(res & (jj >= ii))
        for t in types:
            att[k] = np.where(t, 0.0, NEG)
            k += 1
    qmask = np.zeros((128, 16), np.float64)
    for b in range(4):
        qmask[:, 4 * b + b] = 1.0
    oh4 = np.eye(4)
    att_p = att if hh == 1 else np.full_like(att, NEG)
    return dict(cs_tab=cs, retmask=retmask.astype(f32), cols=cols.astype(f32), attmask=att.astype(f32),
                attmask_p=att_p.astype(f32), qmask=qmask.astype(f32), oh4=oh4.astype(f32))


_CACHE = {}


def kernel(x_prompt, x_sample, state_ret, cache_kv_w128, cache_kv_w512, cache_kv_w2048,
           ln1_g, w_in, ret_gn_g, w_pa, w_pb, w_o, ln2_g, w_ffn_in, w_ffn_out, lnf_g):
    f32 = np.float32
    if "nc" not in _CACHE:
        _CACHE["nc"] = build_program()
    nc = _CACHE["nc"]
    tabs = [_tables(0), _tables(1)]
    shared = dict(
        w_in=np.ascontiguousarray(w_in[0], f32), w_pa=np.ascontiguousarray(w_pa[0], f32),
        w_pb=np.ascontiguousarray(w_pb[0], f32), w_o=np.ascontiguousarray(w_o[0], f32),
        w_fi=np.ascontiguousarray(w_ffn_in[0], f32), w_fo=np.ascontiguousarray(w_ffn_out[0], f32),
        ln1=np.ascontiguousarray(ln1_g[0].reshape(8, 128).T, f32),
        ln2=np.ascontiguousarray(ln2_g[0].reshape(8, 128).T, f32),
        lnf=np.ascontiguousarray(np.broadcast_to(lnf_g[None, :], (128, D)), f32),
        gng=np.ascontiguousarray(ret_gn_g[0].reshape(16, 128).T, f32))
    cs_ = [cache_kv_w128, cache_kv_w512, cache_kv_w2048]
    in_maps = []
    for core in range(_CACHE.get("ncores", 8)):
        m = dict(shared)
        b, hh = core // 2, core % 2
        m.update(tabs[hh])
        if hh == 1:
            m["xp"] = np.ascontiguousarray(x_prompt[b], f32)
        else:
            m["xp"] = np.concatenate([np.zeros((NPRE * 128, D), f32), np.asarray(x_prompt[b, :SEQ - NPRE * 128], f32)], 0)
        m["xs"] = np.ascontiguousarray(x_sample[4 * core:4 * core + 4, 0, :], f32)
        m["st0"] = np.ascontiguousarray(state_ret[0, 4 * core:4 * core + 4], f32)
        for g in range(3):
            m["c%d" % g] = np.ascontiguousarray(cs_[g][0, 4 * core:4 * core + 4], f32)
        in_maps.append(m)
    res = run_bass_kernel_spmd(nc, in_maps, core_ids=list(range(len(in_maps))))
    r = res.results
    if len(r) < 8:
        return r
    y_prompt = np.stack([np.concatenate([r[2 * b]["yp"], r[2 * b + 1]["yp"]], 0) for b in range(4)], 0).astype(f32)
    y_sample = np.concatenate([r[k]["ys"] for k in range(8)], 0).reshape(32, 1, D).astype(f32)
    s_p = np.stack([r[2 * b + 1]["sp_o"] for b in range(4)], 0)[None].astype(f32)
    s_s = np.concatenate([r[k]["ss_o"] for k in range(8)], 0)[None].astype(f32)
    outs = [y_prompt, y_sample, s_p, s_s]
    for g in range(3):
        kvp = np.stack([r[2 * b + 1]["kvp%d" % g] for b in range(4)], 0)[None].astype(f32)
        kvs_ = np.concatenate([r[k]["kvs%d" % g] for k in range(8)], 0).reshape(1, 32, 1, 2, 4, 128).astype(f32)
        outs += [kvp, kvs_]
    return tuple(outs)
```

```python
import math
import os
KDBG = os.environ.get('KDBG', '')
from contextlib import ExitStack
import numpy as np
import concourse.bass as bass
import concourse.mybir as mybir
from concourse.bass_utils import run_bass_kernel_spmd

F32 = mybir.dt.float32
BF16 = mybir.dt.bfloat16
AF = mybir.ActivationFunctionType
ALU = mybir.AluOpType

ENGS = ["pe", "act", "dve", "pool", "sp"]

D = 1024
SEQ = 4096
NCH = SEQ // 128
PAST = 16384
EPS = 1e-6
NEG = -30000.0
GROUPS = ((128, 1), (512, 4), (2048, 16))
RING = (2, 5, 17)
FF = 2816
WS = 132
NPRE = 16
ATT_SCALE = 128.0 ** -0.5


class Prog:
    def __init__(self, nc, es):
        self.nc = nc
        self.es = es
        self.ops = {e: [] for e in ENGS}
        self.cnt = {e: 0 for e in ENGS}
        self.known = {e: {} for e in ENGS}
        self.lastw = {}
        self.reads = {}
        self.sems = {e: es.enter_context(nc.semaphore("c_" + e)) for e in ENGS}
        self.dsem = {}
        self.dval = {}
        self.excl = set()

    def _deps(self, eng, reads, writes):
        toks = []
        for r in reads:
            t = self.lastw.get(r)
            if t is not None:
                toks.append(t)
        for w in writes:
            t = self.lastw.get(w)
            if t is not None:
                toks.append(t)
            toks.extend(self.reads.get(w, []))
        waits = {}
        for (kind, key, val) in toks:
            if kind == "eng" and key == eng and eng == "pe":
                continue
            k = (kind, key)
            if self.known[eng].get(k, 0) >= val:
                continue
            if waits.get(k, 0) < val:
                waits[k] = val
        out = []
        for (kind, key), val in waits.items():
            self.known[eng][(kind, key)] = val
            sem = self.sems[key] if kind == "eng" else self.dsem[key]
            out.append((sem, val))
        return out

    def _register(self, tok, reads, writes):
        for r in reads:
            self.reads.setdefault(r, []).append(tok)
        for w in writes:
            self.lastw[w] = tok
            self.reads[w] = []

    def emit(self, eng, fn, reads=(), writes=(), inc=True):
        if self.excl:
            writes = list(writes) + [r for r in reads if r in self.excl]
            reads = [r for r in reads if r not in self.excl]
        waits = self._deps(eng, reads, writes)
        if inc:
            self.cnt[eng] += 1
            tok = ("eng", eng, self.cnt[eng])
        else:
            assert eng == "pe"
            tok = ("eng", eng, self.cnt[eng] + 1)
        self._register(tok, reads, writes)
        sem = self.sems[eng]

        def run(e, waits=waits, fn=fn, sem=sem, inc=inc):
            for (s, v) in waits:
                e.wait_ge(s, v)
            ins = fn(e)
            if inc:
                ins.then_inc(sem, 1)

        self.ops[eng].append(run)

    def dma(self, eng, key, out, in_, reads=(), writes=(), **kw):
        if key not in self.dsem:
            self.dsem[key] = self.es.enter_context(self.nc.semaphore("d_" + str(key)))
            self.dval[key] = 0
        waits = self._deps(eng, reads, writes)
        self.dval[key] += 16
        tok = ("dma", key, self.dval[key])
        self._register(tok, reads, writes)
        sem = self.dsem[key]

        def run(e, waits=waits, sem=sem, out=out, in_=in_, kw=kw):
            for (s, v) in waits:
                e.wait_ge(s, v)
            e.dma_start(out=out, in_=in_, **kw).then_inc(sem, 16)

        self.ops[eng].append(run)

    def wait_all_dma(self, eng):
        ws = []
        for k, v in self.dval.items():
            if self.known[eng].get(("dma", k), 0) < v:
                ws.append((self.dsem[k], v))
                self.known[eng][("dma", k)] = v

        def run(e, ws=ws):
            for (s, v) in ws:
                e.wait_ge(s, v)

        self.ops[eng].append(run)

    def finish(self, block):
        ops = self.ops

        @block.tensor
        def _(e):
            for f in ops["pe"]:
                f(e)

        @block.scalar
        def _(e):
            for f in ops["act"]:
                f(e)

        @block.vector
        def _(e):
            for f in ops["dve"]:
                f(e)

        @block.gpsimd
        def _(e):
            for f in ops["pool"]:
                f(e)

        @block.sync
        def _(e):
            for f in ops["sp"]:
                f(e)


class StopBuild(Exception):
    pass


def build_program(nchunks=NCH, with_sample=True, stop=99, npre=NPRE):
    def chk(n):
        if n > stop:
            raise StopBuild()

    nc = bass.Bass("TRN2", target_bir_lowering=False)

    def din(name, shape):
        return nc.dram_tensor(name, list(shape), F32, kind="ExternalInput").ap()

    def dout(name, shape):
        return nc.dram_tensor(name, list(shape), F32, kind="ExternalOutput").ap()

    xp = din("xp", [SEQ, D])
    xs = din("xs", [4, D])
    st0 = din("st0", [4, 4, 256, 512])
    caches = [din("c%d" % g, [4, GROUPS[g][0], 2, 4, 128]) for g in range(3)]
    w_in = din("w_in", [D, 12800])
    w_pa = din("w_pa", [2048, D])
    w_pb = din("w_pb", [512, D])
    w_o = din("w_o", [D, D])
    w_fi = din("w_fi", [D, 2 * FF])
    w_fo = din("w_fo", [FF, D])
    ln1 = din("ln1", [128, 8])
    ln2 = din("ln2", [128, 8])
    lnf = din("lnf", [128, D])
    gng = din("gng", [128, 16])
    cs_tab = din("cs_tab", [NCH, 128, 2, WS])
    retmask_d = din("retmask", [4, 128, 128])
    attmask_d = din("attmask", [8, 128, 128])
    attmask_pd = din("attmask_p", [8, 128, 128])
    cols_d = din("cols", [128, 16])
    qmask_d = din("qmask", [128, 16])
    oh4_d = din("oh4", [4, 4])

    yp = dout("yp", [SEQ - npre * 128, D])
    ys = dout("ys", [4, D])
    sp_o = dout("sp_o", [4, 256, 512])
    ss_o = dout("ss_o", [4, 4, 256, 512])
    kvp = [dout("kvp%d" % g, [GROUPS[g][0], 2, 4, 128]) for g in range(3)]
    kvs = [dout("kvs%d" % g, [4, 2, 4, 128]) for g in range(3)]

    gam = [1.0 - 2.0 ** (-5.0 - h) for h in range(4)]
    cdec = [g ** 128 for g in gam]

    with ExitStack() as es:
        def sb(name, shape, dt):
            return es.enter_context(nc.sbuf_tensor(name, list(shape), dt))

        def ps(name, shape, dt):
            return es.enter_context(nc.psum_tensor(name, list(shape), dt))

        xh = sb("xh", [128, 2, D], F32)
        xn_tmp = sb("xn_tmp", [128, D], BF16)
        xnT = sb("xnT", [128, 8, 256], BF16)
        hnT = sb("hnT", [128, 8, 256], BF16)
        q_rT = sb("q_rT", [128, 8, 256], BF16)
        k_rT = sb("k_rT", [128, 8, 256], BF16)
        v_r = sb("v_r", [128, 2, 2048], BF16)
        g_s = sb("g_s", [128, 2, 2048], BF16)
        S = sb("S", [128, 8, 512], F32)
        S_bf = sb("S_bf", [128, 8, 512], BF16)
        o_rT = sb("o_rT", [128, 16, WS], BF16)
        ga_s = sb("ga_s", [128, 8, 256], BF16)
        gb_s = sb("gb_s", [128, 8, 256], BF16)
        mergedT = sb("mergedT", [128, 8, WS], BF16)
        q_aT = sb("q_aT", [128, 12, WS], BF16)
        KT = [sb("KT%d" % g, [128, 4, RING[g] * 128], BF16) for g in range(3)]
        VR = [sb("VR%d" % g, [128, RING[g], 512], BF16) for g in range(3)]
        o_aT = sb("o_aT", [128, 4, WS], BF16)
        actT = sb("actT", [128, 22, 256], BF16)
        wbuf = [sb("wbuf%d" % i, [128, 4096], BF16) for i in range(3)]
        cs = sb("cs", [128, 2, 256], F32)
        retmask = sb("retmask_s", [128, 4, 128], F32)
        attmask = sb("attmask_s", [128, 8, 128], BF16)
        attmask_p = sb("attmask_ps", [128, 8, 128], BF16)
        lnf_bc = sb("lnf_bc", [128, D], F32)
        ident = sb("ident", [128, 128], BF16)
        identf = sb("identf", [128, 128], F32)
        ones = sb("ones", [128, 128], BF16)
        g1T = sb("g1T", [128, 8], F32)
        g2T = sb("g2T", [128, 8], F32)
        gnT = sb("gnT", [128, 16], F32)
        colt = sb("colt", [128, 16], F32)
        qmask = sb("qmask_s", [128, 16], BF16)
        oh4 = sb("oh4_s", [4, 4], F32)
        stat = sb("stat", [128, 8], F32)
        rt = [sb("rt%d" % i, [128, 256], F32) for i in range(4)]
        scmT = sb("scmT", [128, 128], BF16)
        kd = sb("kd", [128, 4, 256], BF16)
        o_g = sb("o_g", [128, 512], BF16)
        PT = sb("PT", [128, 4, 128], BF16)
        PT2 = sb("PT2", [128, 4, 128], BF16)
        rZ = sb("rZ", [128, 128], F32)
        kvst = [sb("kvst%d" % i, [128, 512], F32) for i in range(2)]
        tmpf = sb("tmpf", [128, WS], F32)
        sa_t = sb("sa_t", [128, 256], BF16)
        s0t = sb("s0t", [128, 2, 512], F32)
        snew_bf = sb("snew_bf", [128, 2, 512], BF16)
        qm = sb("qm", [128, 2, 4], BF16)
        Kc = sb("Kc", [128, 512], BF16)
        Vc = sb("Vc", [128, 512], BF16)
        KTs = sb("KTs", [128, 4, 128], BF16)
        PTs = sb("PTs", [128, 4], BF16)
        ksT = sb("ksT", [128, 12, 4], BF16)
        vs_a = sb("vs_a", [4, 3, 512], BF16)
        Ef = sb("Ef", [4, 4], F32)
        Dm = sb("Dm", [4, 4], BF16)

        pT = ps("pT", [128, 8, 128], BF16)
        pf = [ps("pf%d" % i, [128, 2, 256], F32) for i in range(2)]
        pt = [ps("pt%d" % i, [128, 512], F32) for i in range(2)]
        psc = ps("psc", [128, 4, 128], F32)
        pS = ps("pS", [128, 512], F32)
        pUZ = ps("pUZ", [128, 4, 128], F32)

        P = Prog(nc, es)
        P.excl.update(["pT", ("pf", 0), ("pf", 1), ("pt", 0), ("pt", 1), "psc", "pS", "pUZ"])

        def mm(out, lhsT, rhs, start, stop, reads, writes, inc=None, zacc=False):
            if inc is None:
                inc = stop
            if zacc:
                P.emit("pe", lambda e: e.matmul(out, lhsT=lhsT, rhs=rhs, start=start, stop=stop,
                                                skip_group_check=True), reads, writes, inc)
            else:
                P.emit("pe", lambda e: e.matmul(out, lhsT=lhsT, rhs=rhs, start=start, stop=stop),
                       reads, writes, inc)

        def tr(out, in_, idn, reads, writes, inc=True):
            P.emit("pe", lambda e: e.transpose(out=out, in_=in_, identity=idn), reads, writes, inc)

        def act(out, in_, func, reads, writes, scale=None, bias=None, accum=None):
            kw = {}
            if scale is not None:
                kw["scale"] = scale
            if bias is not None:
                kw["bias"] = bias
            if accum is not None:
                kw["accum_out"] = accum
            P.emit("act", lambda e: e.activation(out=out, in_=in_, func=func, **kw), reads, writes)

        def tt(out, in0, in1, op, reads, writes, eng="dve"):
            P.emit(eng, lambda e: e.tensor_tensor(out=out, in0=in0, in1=in1, op=op), reads, writes)

        def tsc(out, in0, s1, op0, reads, writes, s2=None, op1=None, eng="dve"):
            if op1 is None:
                P.emit(eng, lambda e: e.tensor_scalar(out=out, in0=in0, scalar1=s1, scalar2=None, op0=op0),
                       reads, writes)
            else:
                P.emit(eng, lambda e: e.tensor_scalar(out=out, in0=in0, scalar1=s1, scalar2=s2, op0=op0, op1=op1),
                       reads, writes)

        def stt(out, in0, scalar, in1, op0, op1, reads, writes):
            P.emit("dve", lambda e: e.scalar_tensor_tensor(out=out, in0=in0, scalar=scalar, in1=in1,
                                                           op0=op0, op1=op1), reads, writes)

        def recip(out, in_, reads, writes):
            P.emit("dve", lambda e: e.reciprocal(out=out, in_=in_), reads, writes)

        def cp(out, in_, reads, writes, eng="dve"):
            P.emit(eng, lambda e: e.tensor_copy(out=out, in_=in_), reads, writes)

        P.dma("sp", "k_ret", retmask[:], retmask_d.rearrange("h j i -> j h i"), writes=["retmask"])
        P.dma("pool", "k_att", attmask[:], attmask_d.rearrange("m j i -> j m i"), writes=["attmask"])
        P.dma("pool", "k_attp", attmask_p[:], attmask_pd.rearrange("m j i -> j m i"), writes=["attmask_p"])
        P.dma("sp", "k_lnf", lnf_bc[:], lnf[:, :], writes=["lnf_bc"])
        P.dma("sp", "k_g1", g1T[:], ln1[:, :], writes=["g1T"])
        P.dma("sp", "k_g2", g2T[:], ln2[:, :], writes=["g2T"])
        P.dma("sp", "k_gn", gnT[:], gng[:, :], writes=["gnT"])
        P.dma("sp", "k_col", colt[:], cols_d[:, :], writes=["colt"])
        P.dma("pool", "k_qm", qmask[:], qmask_d[:, :], writes=["qmask"])
        P.dma("sp", "k_oh", oh4[:], oh4_d[:, :], writes=["oh4"])
        P.emit("pool", lambda e: e.memset(identf[:], 0.0), writes=["identf"])
        P.emit("pool", lambda e: e.affine_select(out=identf[:], in_=identf[:], pattern=[[-1, 128]],
                                                 compare_op=ALU.not_equal, fill=1.0, base=0,
                                                 channel_multiplier=1),
               reads=["identf"], writes=["identf"])
        cp(ident[:], identf[:], ["identf"], ["ident"])
        P.emit("pool", lambda e: e.memset(ones[:], 1.0), writes=["ones"])

        wq = []
        wstate = {"issued": 0, "cur": -1}

        def w_issue_upto(n):
            while wstate["issued"] <= n and wstate["issued"] < len(wq):
                i = wstate["issued"]
                slot = i % 3
                j = wq[i]
                if j not in converted:
                    conv_upto(NU)
                ncols = 2816 if j >= 43 else 4096
                P.dma("sp", "w%d" % slot, wbuf[slot][:, 0:ncols], scr[j][:, 0:ncols],
                      reads=[("scr", j, 0), ("scr", j, 1)], writes=[("wbuf", slot, 0), ("wbuf", slot, 1)])
                wstate["issued"] += 1

        def w_next():
            wstate["cur"] += 1
            n = wstate["cur"]
            w_issue_upto(n + 2)
            return wbuf[n % 3], [("wbuf", n % 3, 0), ("wbuf", n % 3, 1)]

        def v3(buf, a, b):
            return buf[:, 0:a * b].rearrange("p (a b) -> p a b", b=b)

        def unit_win(u):
            return lambda buf: [(v3(buf, 8, 512), w_in[:, 512 * u:512 * (u + 1)].rearrange("(kc p) n -> p kc n", p=128))]

        def unit_wpa(u):
            return lambda buf: [(v3(buf, 16, 256), w_pa[:, 256 * u:256 * (u + 1)].rearrange("(kc p) n -> p kc n", p=128))]

        def unit_wpb():
            return lambda buf: [(v3(buf, 4, 1024), w_pb[:, :].rearrange("(kc p) n -> p kc n", p=128))]

        def unit_wo(u):
            return lambda buf: [(v3(buf, 8, 512), w_o[:, 512 * u:512 * (u + 1)].rearrange("(kc p) n -> p kc n", p=128))]

        def unit_fi(u):
            def f(buf):
                va = buf[:, 0:2048].rearrange("p (a b) -> p a b", b=256)
                vb = buf[:, 2048:4096].rearrange("p (a b) -> p a b", b=256)
                return [(va, w_fi[:, 256 * u:256 * (u + 1)].rearrange("(kc p) n -> p kc n", p=128)),
                        (vb, w_fi[:, FF + 256 * u:FF + 256 * (u + 1)].rearrange("(kc p) n -> p kc n", p=128))]
            return f

        def unit_fo(hh, q):
            return lambda buf: [(v3(buf, 11, 256), w_fo[hh * 1408:(hh + 1) * 1408, 256 * q:256 * (q + 1)].rearrange("(kc p) n -> p kc n", p=128))]

        def iter_has_kout(c, g):
            return c >= NCH - GROUPS[g][0] // 128

        unit_defs = ([unit_win(u) for u in range(25)] + [unit_wpa(u) for u in range(4)] + [unit_wpb()]
                     + [unit_wo(u) for u in range(2)] + [unit_fi(u) for u in range(11)]
                     + [unit_fo(hh, q) for hh in range(2) for q in range(4)])
        NU = len(unit_defs)
        assert NU == 51
        scr = nc.dram_tensor("wscr", [NU, 128, 4096], BF16, kind="Internal").ap()
        conv_order = [2, 3, 4, 5, 6, 7, 17, 20, 16, 19, 15, 18] + \
                     [j for j in range(NU) if j not in (2, 3, 4, 5, 6, 7, 15, 16, 17, 18, 19, 20)]
        conv_state = {"n": 0}
        converted = set()

        def conv_upto(n):
            while conv_state["n"] < min(n, NU):
                j = conv_order[conv_state["n"]]
                parts = unit_defs[j](scr[j])
                for pi, (dst, src) in enumerate(parts):
                    wr = [("scr", j, 0), ("scr", j, 1)] if len(parts) == 1 else [("scr", j, pi)]
                    P.dma("pool", "cv%d" % j, dst, src, writes=wr)
                converted.add(j)
                conv_state["n"] += 1
        SHARED_UNITS = [0, 1, 2, 3, 4, 5, 6, 7, 8, 9, 10, 11, 21, 22, 23, 24]

        def units_for(c):
            if c >= npre:
                return list(range(32))
            need = [g for g in range(3) if c >= npre - GROUPS[g][0] // 128]
            return [2, 3, 4, 5, 6, 7] + [15 + g for g in need] + [18 + g for g in need]

        plan = []
        c_ = 0
        while c_ < nchunks:
            if c_ + 1 < npre and c_ + 1 < nchunks:
                plan.append(("body2", c_, c_ + 1))
                c_ += 2
            elif c_ < npre:
                plan.append(("body", c_, 0))
                c_ += 1
            elif c_ == npre or c_ == nchunks - 1:
                plan.append(("body", c_, 0))
                fl_ = [(0, 128, 0, "p", c_ - npre)]
                if with_sample and c_ == npre:
                    fl_.append((1, 4, 128, "s", 0))
                plan.append(("ffn", fl_))
                c_ += 1
            else:
                plan.append(("shared", c_, c_ + 1))
                plan.append(("rest", c_, 0))
                plan.append(("rest", c_ + 1, 1))
                plan.append(("ffn", [(0, 128, 0, "p", c_ - npre), (1, 128, 128, "p", c_ + 1 - npre)]))
                c_ += 2
        for it in plan:
            if it[0] == "body":
                wq.extend(units_for(it[1]))
            elif it[0] == "body2":
                wq.extend(units_for(it[2]))
            elif it[0] == "shared":
                wq.extend(SHARED_UNITS)
            elif it[0] == "rest":
                wq.extend([u_ for u_ in range(32) if u_ not in SHARED_UNITS])
            else:
                wq.extend(range(32, NU))

        def norm_T(k, R, col0, gT, gkey, dstT, dkey):
            act(xn_tmp[:R, :], xh[:R, k, :], AF.Square, [("xh", k)], [("xn_tmp", 0), ("xn_tmp", 1), ("stat", k)],
                accum=stat[:R, k:k + 1])
            act(stat[:R, 2 + k:3 + k], stat[:R, k:k + 1], AF.Sqrt, [("stat", k), "colt"], [("stat", 2 + k)],
                scale=1.0 / D, bias=colt[:R, 8:9])
            recip(stat[:R, 4 + k:5 + k], stat[:R, 2 + k:3 + k], [("stat", 2 + k)], [("stat", 4 + k)])
            for hf in range(2):
                tsc(xn_tmp[:R, hf * 512:(hf + 1) * 512], xh[:R, k, hf * 512:(hf + 1) * 512], stat[:R, 4 + k:5 + k],
                    ALU.mult, [("xh", k), ("stat", 4 + k)], [("xn_tmp", hf)])
            for kc in range(8):
                tr(pT[:, kc, 0:R], xn_tmp[:R, kc * 128:(kc + 1) * 128], ident[:R, :R],
                   [("xn_tmp", kc // 4), "ident"], ["pT"], inc=(kc == 7))
            for kc in range(8):
                if kc % 2 == 0:
                    act(dstT[:, kc, col0:col0 + R], pT[:, kc, 0:R], AF.Copy, ["pT", gkey], [dkey],
                        scale=gT[:, kc:kc + 1])
                else:
                    tsc(dstT[:, kc, col0:col0 + R], pT[:, kc, 0:R], gT[:, kc:kc + 1], ALU.mult, ["pT", gkey], [dkey])

        pfi = [0]
        pti = [0]

        def next_pf():
            pfi[0] ^= 1
            return pf[pfi[0]], ("pf", pfi[0])

        def next_pt():
            pti[0] ^= 1
            return pt[pti[0]], ("pt", pti[0])

        kvi = [0]

        def next_kvst():
            kvi[0] ^= 1
            return kvst[kvi[0]], ("kvst", kvi[0])

        def body(c, kx, c2=None, phase=None, pk0=0):
            prefix = c < npre
            pair = c2 is not None
            sample = with_sample and c == npre and phase is None
            ulist = units_for(c2 if pair else c)
            if phase == "shared":
                ulist = SHARED_UNITS
            elif phase == "rest":
                ulist = [u_ for u_ in range(32) if u_ not in SHARED_UNITS]
            W = 256 if pair else (WS if sample else 128)
            chunks = [(0, 128, 0)] + ([(1, 4, 128)] if sample else [])
            if pair:
                chunks = [(0, 128, 0), (1, 128, 128)]
            if phase == "rest":
                chunks = [(pk0, 128, 0)]
            sc = 128 * pk0 if phase == "rest" else 0
            pchunks = [t for t in chunks if t[1] == 128]
            cids = {0: c, 1: c2}
            if phase == "rest":
                cids = {pk0: c}
            xs_of = {0: kx, 1: 1}
            cskeys = [("cs", 0), ("cs", 1)]

            if phase != "rest":
                P.dma("pool", "x%d" % kx, xh[:, kx, :], xp[c * 128:(c + 1) * 128, :], writes=[("xh", kx)])
            if phase == "rest":
                pass
            elif pair:
                P.dma("pool", "x1", xh[:, 1, :], xp[c2 * 128:(c2 + 1) * 128, :], writes=[("xh", 1)])
                P.dma("pool", "cs0", cs[:, :, 0:128], cs_tab[c][:, :, 0:128], writes=[("cs", 0)])
                P.dma("pool", "cs1", cs[:, :, 128:256], cs_tab[c2][:, :, 0:128], writes=[("cs", 1)])
            else:
                P.dma("pool", "cs0", cs[:, :, 0:W], cs_tab[c][:, :, 0:W], writes=cskeys)
            if sample:
                P.dma("pool", "x1", xh[:4, 1, :], xs[:, :], writes=[("xh", 1)])
            if phase != "rest":
                conv_upto(12 if conv_state["n"] == 0 else conv_state["n"] + 7)
            chk(1)
            for (k, R, col0) in (chunks if phase != "rest" else []):
                norm_T(xs_of[k], R, col0, g1T, "g1T", xnT, "xnT")

            chk(2)
            def feat_unit(wb, wkey, nf, evac):
                for half in range(nf // 2):
                    ptile, pkey = next_pf()
                    for fl in range(2):
                        f = half * 2 + fl
                        for kc in range(8):
                            if 'nomm' in KDBG:
                                continue
                            mm(ptile[:, fl, 0:W], v3(wb, 8, 512)[:, kc, f * 128:(f + 1) * 128], xnT[:, kc, sc:sc + W],
                               kc == 0, kc == 7, wkey + ["xnT"], [pkey])
                    if 'noevac' not in KDBG:
                        evac(half, ptile, pkey)

            def tok_unit(wb, wkey, evac):
                for (k, R, col0) in chunks:
                    ptile, pkey = next_pt()
                    for kc in range(8):
                        mm(ptile[:R, :], xnT[:, kc, sc + col0:sc + col0 + R], v3(wb, 8, 512)[:, kc, :],
                           kc == 0, kc == 7, wkey + ["xnT"], [pkey])
                    evac(k, R, ptile, pkey)

            for u in range(25):
                chk(2 + (u + 1) / 100.0)
                if u not in ulist:
                    continue
                wb, wkey = w_next()
                if u < 4:
                    dstT, dkey = (q_rT, "q_rT") if u < 2 else (k_rT, "k_rT")
                    hbase = (u % 2) * 2

                    def evac_rot(half, ptile, pkey, dstT=dstT, dkey=dkey, hbase=hbase):
                        h = hbase + half
                        A = ptile[:, 0, 0:W]
                        B = ptile[:, 1, 0:W]
                        cosv = cs[:, 0, 0:W]
                        sinv = cs[:, 1, 0:W]
                        tt(rt[0][:, 0:W], A, cosv, ALU.mult, [pkey] + cskeys, [("rt", 0)])
                        tt(rt[1][:, 0:W], B, sinv, ALU.mult, [pkey] + cskeys, [("rt", 1)])
                        tt(rt[2][:, 0:W], A, sinv, ALU.mult, [pkey] + cskeys, [("rt", 2)])
                        tt(rt[3][:, 0:W], B, cosv, ALU.mult, [pkey] + cskeys, [("rt", 3)])
                        tt(dstT[:, 2 * h, 0:W], rt[0][:, 0:W], rt[1][:, 0:W], ALU.subtract,
                           [("rt", 0), ("rt", 1)], [dkey])
                        tt(dstT[:, 2 * h + 1, 0:W], rt[2][:, 0:W], rt[3][:, 0:W], ALU.add,
                           [("rt", 2), ("rt", 3)], [dkey])
                    feat_unit(wb, wkey, 4, evac_rot)
                elif u < 12:
                    isv = u < 8
                    ucol = (u - 4) % 4

                    def evac_vg(k, R, ptile, pkey, isv=isv, ucol=ucol):
                        if isv:
                            act(v_r[:R, k, ucol * 512:(ucol + 1) * 512], ptile[:R, :], AF.Copy, [pkey], [("v_r", k)])
                        else:
                            act(g_s[:R, k, ucol * 512:(ucol + 1) * 512], ptile[:R, :], AF.Silu, [pkey], [("g_s", k)])
                    tok_unit(wb, wkey, evac_vg)
                elif u < 15:
                    g = u - 12

                    def evac_qa(half, ptile, pkey, g=g):
                        for fl in range(2):
                            act(q_aT[:, g * 4 + half * 2 + fl, 0:W], ptile[:, fl, 0:W], AF.Copy, [pkey], ["q_aT"])
                    feat_unit(wb, wkey, 4, evac_qa)
                elif u < 18:
                    g = u - 15

                    def evac_ka(half, ptile, pkey, g=g):
                        for fl in range(2):
                            h = half * 2 + fl
                            for (pk, pR, pcol) in pchunks:
                                slot = cids[pk] % RING[g]
                                act(KT[g][:, h, slot * 128:(slot + 1) * 128], ptile[:, fl, pcol:pcol + 128], AF.Copy,
                                    [pkey], [("KT", g, slot)])
                            if sample:
                                act(ksT[:, g * 4 + h, 0:4], ptile[:, fl, 128:132], AF.Copy, [pkey], ["ksT"])
                    feat_unit(wb, wkey, 4, evac_ka)
                    if iter_has_kout(c, g) or sample:
                        def evac_kout(k, R, ptile, pkey, g=g):
                            if R == 128 and not iter_has_kout(cids[k], g):
                                return
                            st, skey = next_kvst()
                            cp(st[:R, :], ptile[:R, :], [pkey], [skey])
                            if R == 128:
                                r0 = cids[k] * 128 - (SEQ - GROUPS[g][0])
                                P.dma("sp", "st_" + str(skey[1]), kvp[g][r0:r0 + 128, 0, :, :],
                                      st[:, :].rearrange("p (h e) -> p h e", e=128), reads=[skey])
                            else:
                                P.dma("sp", "st_" + str(skey[1]), kvs[g][:, 0, :, :],
                                      st[:4, :].rearrange("p (h e) -> p h e", e=128), reads=[skey])
                        tok_unit(wb, wkey, evac_kout)
                elif u < 21:
                    g = u - 18

                    def evac_va(k, R, ptile, pkey, g=g):
                        if R == 128:
                            slot = cids[k] % RING[g]
                            act(VR[g][:, slot, :], ptile[:, :], AF.Copy, [pkey], [("VR", g, slot)])
                        elif 'nova1' not in KDBG:
                            act(vs_a[:4, g, :], ptile[:4, :], AF.Copy, [pkey], ["vs_a"])
                        if ((R == 128 and iter_has_kout(cids[k], g)) or R == 4) and 'nova2' not in KDBG:
                            st, skey = next_kvst()
                            cp(st[:R, :], ptile[:R, :], [pkey], [skey])
                            if R == 128:
                                r0 = cids[k] * 128 - (SEQ - GROUPS[g][0])
                                P.dma("sp", "st_" + str(skey[1]), kvp[g][r0:r0 + 128, 1, :, :],
                                      st[:, :].rearrange("p (h e) -> p h e", e=128), reads=[skey])
                            else:
                                P.dma("sp", "st_" + str(skey[1]), kvs[g][:, 1, :, :],
                                      st[:4, :].rearrange("p (h e) -> p h e", e=128), reads=[skey])
                    tok_unit(wb, wkey, evac_va)
                else:
                    dst, dkey = (ga_s, "ga_s") if u < 23 else (gb_s, "gb_s")
                    fbase = ((u - 21) % 2) * 4

                    def evac_gate(half, ptile, pkey, dst=dst, dkey=dkey, fbase=fbase):
                        for fl in range(2):
                            act(dst[:, fbase + half * 2 + fl, 0:W], ptile[:, fl, 0:W], AF.Sigmoid, [pkey], [dkey])
                    feat_unit(wb, wkey, 4, evac_gate)

            if phase == "shared":
                return
            chk(3)
            def gn_and_T(po, pokey, R, k, col0, h, epsc):
                gn_B(po, pokey, R, k, col0, h, epsc)
                gn_C(R, col0, h)

            def gn_B(po, pokey, R, k, col0, h, epsc):
                act(xn_tmp[:R, 0:512], po[:R, :], AF.Square, [pokey], [("xn_tmp", 0), ("stat", 6)], accum=stat[:R, 6:7])
                act(stat[:R, 7:8], stat[:R, 6:7], AF.Sqrt, [("stat", 6), "colt"], [("stat", 7)],
                    scale=1.0 / 512.0, bias=epsc)
                recip(stat[:R, 6:7], stat[:R, 7:8], [("stat", 7)], [("stat", 6)])
                stt(o_g[:R, :], po[:R, :], stat[:R, 6:7], g_s[:R, k, h * 512:(h + 1) * 512], ALU.mult, ALU.mult,
                    [pokey, ("stat", 6), ("g_s", k)], ["o_g"])

            pT2 = pf[1][:, :, :].bitcast(BF16)[:, 0, :].rearrange("p (a b) -> p a b", b=128)
            pT2key = ("pf", 1)

            def gn_C(R, col0, h):
                for vq in range(4):
                    tr(pT2[:, vq, 0:R], o_g[:R, vq * 128:(vq + 1) * 128], ident[:R, :R], ["o_g", "ident"], [pT2key],
                       inc=(vq == 3))
                for vq in range(4):
                    if vq % 2 == 0:
                        act(o_rT[:, h * 4 + vq, col0:col0 + R], pT2[:, vq, 0:R], AF.Copy, [pT2key, "gnT"], ["o_rT"],
                            scale=gnT[:, h * 4 + vq:h * 4 + vq + 1])
                    else:
                        tsc(o_rT[:, h * 4 + vq, col0:col0 + R], pT2[:, vq, 0:R], gnT[:, h * 4 + vq:h * 4 + vq + 1],
                            ALU.mult, [pT2key, "gnT"], ["o_rT"])

            for (pk, pR, pcol) in pchunks:
              cid = cids[pk]
              for h in range(4):
                for X in range(2):
                    tr(pT[:, X, :], k_rT[:, 2 * h + X, sc + pcol:sc + pcol + 128], ident[:, :], ["k_rT", "ident"], ["pT"], inc=(X == 1))
                act(kd[:, h, :], pT[:, 0:2, :].rearrange("p a b -> p (a b)"), AF.Copy, ["pT", "colt"], [("kd", h)],
                    scale=colt[:, h:h + 1])
                if not prefix:
                    for X in range(2):
                        mm(psc[:, 0, :], k_rT[:, 2 * h + X, sc + pcol:sc + pcol + 128], q_rT[:, 2 * h + X, sc:sc + 128], X == 0, X == 1,
                           ["k_rT", "q_rT"], ["psc"])
                    tt(scmT[:, :], psc[:, 0, :], retmask[:, h, :], ALU.mult, ["psc", "retmask"], ["scmT"])
                for X in range(2):
                    pb, pbkey = [(pS, "pS"), (pf[0][:, :, :].rearrange("p a b -> p (a b)"), ("pf", 0))][X]
                    mm(pb[:, :], kd[:, h, X * 128:(X + 1) * 128], v_r[:, pk, h * 512:(h + 1) * 512], True, True,
                       [("kd", h), ("v_r", pk)], [pbkey])
                    if cid == 0:
                        cp(S[:, 2 * h + X, :], pb[:, :], [pbkey], [("S", h)])
                    else:
                        stt(S[:, 2 * h + X, :], S[:, 2 * h + X, :], float(cdec[h]), pb[:, :], ALU.mult, ALU.add,
                            [pbkey, ("S", h)], [("S", h)])
                if not prefix:
                    po, pokey = next_pt()
                    last_is_intra = (cid == 0)
                    mm(po[:, :], scmT[:, :], v_r[:, pk, h * 512:(h + 1) * 512], True, last_is_intra,
                       ["scmT", ("v_r", pk)], [pokey])
                    if cid > 0:
                        for X in range(2):
                            mm(po[:, :], q_rT[:, 2 * h + X, sc:sc + 128], S_bf[:, 2 * h + X, :], False, X == 1,
                               ["q_rT", ("S_bf", h)], [pokey])
                if npre - 1 <= cid < nchunks - 1:
                    for X in range(2):
                        act(S_bf[:, 2 * h + X, :], S[:, 2 * h + X, :], AF.Copy, [("S", h)], [("S_bf", h)])
                if not prefix:
                    if h > 0:
                        gn_C(128, 0, h - 1)
                    gn_B(po, pokey, 128, pk, 0, h, colt[:, 4 + h:5 + h])
                    if h == 3:
                        gn_C(128, 0, 3)

            if prefix:
                return
            chk(3.5)
            if sample:
                for h in range(4):
                    for X in range(2):
                        tr(pT[:4, X, :], k_rT[:, 2 * h + X, 128:132], ident[:, :], ["k_rT", "ident"], ["pT"],
                           inc=(X == 1))
                    for b in range(4):
                        act(kd[:4, b, :], pT[:4, 0:2, :].rearrange("p a b -> p (a b)"), AF.Copy, ["pT", "colt"],
                            [("kd", b)], scale=colt[:4, 9 + b:10 + b])
                    po, pokey = next_pt()
                    for b in range(4):
                        for X in range(2):
                            P.dma("sp", "s0_%d" % X, s0t[:, X, :], st0[b, h, X * 128:(X + 1) * 128, :],
                                  writes=[("s0t", X)])
                        for X in range(2):
                            tt(qm[:, X, :], q_rT[:, 2 * h + X, 128:132], qmask[:, 4 * b:4 * b + 4], ALU.mult,
                               ["q_rT", "qmask"], [("qm", X)])
                        for X in range(2):
                            pb, pbkey = [(pS, "pS"), (pf[0][:, :, :].rearrange("p a b -> p (a b)"), ("pf", 0))][X]
                            mm(pb[:, :], kd[:4, b, X * 128:(X + 1) * 128], v_r[:4, 1, h * 512:(h + 1) * 512],
                               True, True, [("kd", b), ("v_r", 1)], [pbkey])
                            stt(s0t[:, X, :], s0t[:, X, :], float(gam[h]), pb[:, :], ALU.mult, ALU.add,
                                [pbkey, ("s0t", X)], [("s0t", X)])
                            act(snew_bf[:, X, :], s0t[:, X, :], AF.Copy, [("s0t", X)], [("snew_bf", X)])
                            P.dma("sp", "s0o_%d" % X, ss_o[b, h, X * 128:(X + 1) * 128, :], s0t[:, X, :],
                                  reads=[("s0t", X)])
                        for X in range(2):
                            mm(po[:4, :], qm[:, X, :], snew_bf[:, X, :], b == 0 and X == 0, b == 3 and X == 1,
                               [("qm", X), ("snew_bf", X)], [pokey])
                    gn_and_T(po, pokey, 4, 1, 128, h, colt[:4, 8:9])

            chk(4)
            for u in range(4):
                wb, wkey = w_next()
                ptile, pkey = next_pf()
                for fl in range(2):
                    for vc in range(16):
                        mm(ptile[:, fl, 0:W], v3(wb, 16, 256)[:, vc, fl * 128:(fl + 1) * 128], o_rT[:, vc, 0:W],
                           vc == 0, vc == 15, wkey + ["o_rT"], [pkey])
                for fl in range(2):
                    f = u * 2 + fl
                    tt(mergedT[:, f, 0:W], ptile[:, fl, 0:W], ga_s[:, f, sc:sc + W], ALU.mult, [pkey, "ga_s"],
                       [("mergedT", f)])

            chk(5)
            sc_tiles = [(psc, "psc"),
                        (pf[0][:, :, :].rearrange("p a (b c) -> p (a b) c", c=128), ("pf", 0)),
                        (pf[1][:, :, :].rearrange("p a (b c) -> p (a b) c", c=128), ("pf", 1))]
            pt_bufs = [(PT, "PT"), (PT2, "PT2")]
            uz_tiles = [(pUZ, "pUZ"), (pS[:, :].rearrange("p (a b) -> p a b", b=128), "pS")]

            def att_head(s, groups, nblk, uz, uzkey):
                def memz(e, uz=uz):
                    return e.memset(uz[:, 0:2, :], 0.0)
                P.emit("dve", memz, writes=[uzkey])

                def emit_S(i):
                    tile, tkey = sc_tiles[i % 3]
                    grp = groups[i]
                    for q, (g, slot, mi) in enumerate(grp):
                        mm(tile[:, q, :], KT[g][:, s, slot * 128:(slot + 1) * 128], q_aT[:, g * 4 + s, 0:128],
                           True, False, [("KT", g, slot), "q_aT"], [tkey], inc=False)
                        mtile, mkey = (attmask_p, "attmask_p") if mi[1] else (attmask, "attmask")
                        mm(tile[:, q, :], ident[:, :], mtile[:, mi[0], :], False, True, ["ident", mkey], [tkey],
                           inc=(q == len(grp) - 1))

                def emit_E(i):
                    tile, tkey = sc_tiles[i % 3]
                    pb, pbkey = pt_bufs[i % 2]
                    n = len(groups[i])
                    act(pb[:, 0:n, :], tile[:, 0:n, :], AF.Exp, [tkey], [pbkey], scale=ATT_SCALE)

                def emit_PV(i):
                    pb, pbkey = pt_bufs[i % 2]
                    grp = groups[i]
                    for q, (g, slot, mi) in enumerate(grp):
                        last = (i * 4 + q == nblk - 1)
                        mm(uz[:, 0, :], VR[g][:, slot, s * 128:(s + 1) * 128], pb[:, q, :], False, last,
                           [("VR", g, slot), pbkey], [uzkey], inc=False, zacc=True)
                        mm(uz[:, 1, :], ones[:, :], pb[:, q, :], False, last, ["ones", pbkey], [uzkey],
                           inc=(q == len(grp) - 1), zacc=True)

                emit_S(0)
                for i in range(len(groups)):
                    if i + 1 < len(groups):
                        emit_S(i + 1)
                    emit_E(i)
                    emit_PV(i)
                recip(rZ[:, :], uz[:, 1, :], [uzkey], ["rZ"])
                tt(o_aT[:, s, 0:128], uz[:, 0, :], rZ[:, :], ALU.mult, [uzkey, "rZ"], ["o_aT"])

            for s in range(4):
                blocks = []
                for g, (win, dil) in enumerate(GROUPS):
                    nb = win // 128
                    for kb in range(c - nb, c + 1):
                        if kb < 0:
                            continue
                        if kb == c:
                            mtype = 0
                        elif kb == c - nb:
                            mtype = 2
                        else:
                            mtype = 1
                        mi = {0: (0, None, 1), 1: (2, 3, 4), 2: (5, 6, 7)}[g][mtype]
                        blocks.append((g, kb % RING[g], (mi, kb < npre)))
                nblk = len(blocks)
                groups = [blocks[b0:b0 + 4] for b0 in range(0, nblk, 4)]
                uz, uzkey = uz_tiles[s % 2]
                att_head(s, groups, nblk, uz, uzkey)

            chk(5.5)
            if sample:
                P.emit("dve", lambda e: e.memset(pUZ[:, 2, 0:32], 0.0), writes=["pUZ"])
                for s in range(4):
                    for g in range(3):
                        mm(psc[:4, 0, 0:4], ksT[:, g * 4 + s, 0:4], q_aT[:, g * 4 + s, 128:132], True, True,
                           ["ksT", "q_aT"], ["psc"])
                        act(Ef[:, :], psc[:4, 0, 0:4], AF.Exp, ["psc"], ["Ef"], scale=ATT_SCALE)
                        tt(Dm[:, :], Ef[:, :], oh4[:, :], ALU.mult, ["Ef", "oh4"], ["Dm"])
                        mm(pUZ[:, 2, s * 4:s * 4 + 4], vs_a[:4, g, s * 128:(s + 1) * 128], Dm[:, :], False, False,
                           ["vs_a", "Dm"], ["pUZ"], inc=False, zacc=True)
                        mm(pUZ[:, 2, 16 + s * 4:16 + s * 4 + 4], ones[:4, :], Dm[:, :], False, False,
                           ["ones", "Dm"], ["pUZ"], inc=True, zacc=True)
                for b in range(4):
                    for g, (win, dil) in enumerate(GROUPS):
                        P.dma("pool", "kc", Kc[:, :].rearrange("p (h e) -> p h e", e=128),
                              caches[g][b, 0:win:dil, 0, :, :], writes=["Kc"])
                        P.dma("pool", "vc", Vc[:, :].rearrange("p (h e) -> p h e", e=128),
                              caches[g][b, 0:win:dil, 1, :, :], writes=["Vc"])
                        for s in range(4):
                            tr(pT[:, s, :], Kc[:, s * 128:(s + 1) * 128], ident[:, :], ["Kc", "ident"], ["pT"],
                               inc=(s == 3))
                        act(KTs[:, :, :], pT[:, 0:4, :], AF.Copy, ["pT"], ["KTs"])
                        for s in range(4):
                            mm(psc[:, 1, s:s + 1], KTs[:, s, :], q_aT[:, g * 4 + s, 128 + b:129 + b], True, True,
                               ["KTs", "q_aT"], ["psc"], inc=(s == 3))
                        act(PTs[:, :], psc[:, 1, 0:4], AF.Exp, ["psc"], ["PTs"], scale=ATT_SCALE)
                        for s in range(4):
                            last = (g == 2)
                            mm(pUZ[:, 2, s * 4 + b:s * 4 + b + 1], Vc[:, s * 128:(s + 1) * 128], PTs[:, s:s + 1],
                               False, last, ["Vc", "PTs"], ["pUZ"], inc=False, zacc=True)
                            mm(pUZ[:, 2, 16 + s * 4 + b:16 + s * 4 + b + 1], ones[:, :], PTs[:, s:s + 1],
                               False, last, ["ones", "PTs"], ["pUZ"], inc=True, zacc=True)
                recip(rZ[:, 0:16], pUZ[:, 2, 16:32], ["pUZ"], ["rZ"])
                for s in range(4):
                    tt(o_aT[:, s, 128:132], pUZ[:, 2, s * 4:s * 4 + 4], rZ[:, s * 4:s * 4 + 4], ALU.mult,
                       ["pUZ", "rZ"], ["o_aT"])

            chk(6)
            wb, wkey = w_next()
            for half in range(4):
                ptile, pkey = next_pf()
                for fl in range(2):
                    f = half * 2 + fl
                    for sl in range(4):
                        mm(ptile[:, fl, 0:W], v3(wb, 4, 1024)[:, sl, f * 128:(f + 1) * 128], o_aT[:, sl, 0:W],
                           sl == 0, sl == 3, wkey + ["o_aT"], [pkey])
                for fl in range(2):
                    f = half * 2 + fl
                    tt(tmpf[:, 0:W], ptile[:, fl, 0:W], gb_s[:, f, sc:sc + W], ALU.mult, [pkey, "gb_s"], ["tmpf"])
                    tt(mergedT[:, f, 0:W], tmpf[:, 0:W], mergedT[:, f, 0:W], ALU.add, ["tmpf", ("mergedT", f)],
                       [("mergedT", f)])

            chk(7)
            for u in range(2):
                wb, wkey = w_next()
                for (k, R, col0) in chunks:
                    ptile, pkey = next_pt()
                    for kc in range(8):
                        mm(ptile[:R, :], mergedT[:, kc, col0:col0 + R], v3(wb, 8, 512)[:, kc, :], kc == 0, kc == 7,
                           wkey + [("mergedT", kc)], [pkey])
                    kk = xs_of[k]
                    tt(xh[:R, kk, u * 512:(u + 1) * 512], ptile[:R, :], xh[:R, kk, u * 512:(u + 1) * 512], ALU.add,
                       [pkey, ("xh", kk)], [("xh", kk)])

        def ffn(fchunks):
            chunks = [(k, R, col0) for (k, R, col0, kind, row) in fchunks]
            W = max(col0 + R for (k, R, col0) in chunks)
            chk(8)
            for (k, R, col0) in chunks:
                norm_T(k, R, col0, g2T, "g2T", hnT, "hnT")

            chk(9)
            for u in range(11):
                wb, wkey = w_next()
                for fl in range(2):
                    ptile, pkey = next_pf()
                    for ab in range(2):
                        for kc in range(8):
                            wpart = wb[:, ab * 2048:(ab + 1) * 2048].rearrange("p (a b) -> p a b", b=256)
                            mm(ptile[:, ab, 0:W], wpart[:, kc, fl * 128:(fl + 1) * 128],
                               hnT[:, kc, 0:W], kc == 0, kc == 7, wkey + ["hnT"], [pkey])
                    act(sa_t[:, 0:W], ptile[:, 0, 0:W], AF.Silu, [pkey], ["sa_t"])
                    tt(actT[:, 2 * u + fl, 0:W], ptile[:, 1, 0:W], sa_t[:, 0:W], ALU.mult, [pkey, "sa_t"],
                       [("actT", 2 * u + fl)])

            chk(10)
            for hh in range(2):
                for q in range(4):
                    wb, wkey = w_next()
                    for (k, R, col0) in chunks:
                        ptile, pkey = next_pt()
                        for hl in range(11):
                            hc = hh * 11 + hl
                            mm(ptile[:R, 0:256], actT[:, hc, col0:col0 + R], v3(wb, 11, 256)[:, hl, :],
                               hl == 0, hl == 10, wkey + [("actT", hc)], [pkey])
                        tt(xh[:R, k, q * 256:(q + 1) * 256], ptile[:R, 0:256], xh[:R, k, q * 256:(q + 1) * 256],
                           ALU.add, [pkey, ("xh", k)], [("xh", k)])

            chk(11)
            for (k, R, col0, kind, row) in fchunks:
                act(xn_tmp[:R, :], xh[:R, k, :], AF.Square, [("xh", k)], [("xn_tmp", 0), ("xn_tmp", 1), ("stat", k)],
                    accum=stat[:R, k:k + 1])
                act(stat[:R, 2 + k:3 + k], stat[:R, k:k + 1], AF.Sqrt, [("stat", k), "colt"], [("stat", 2 + k)],
                    scale=1.0 / D, bias=colt[:R, 8:9])
                recip(stat[:R, 4 + k:5 + k], stat[:R, 2 + k:3 + k], [("stat", 2 + k)], [("stat", 4 + k)])
                stt(xh[:R, k, :], xh[:R, k, :], stat[:R, 4 + k:5 + k], lnf_bc[:R, :], ALU.mult, ALU.mult,
                    [("xh", k), ("stat", 4 + k), "lnf_bc"], [("xh", k)])
                if kind == "p":
                    P.dma("sp", "y%d" % k, yp[row * 128:(row + 1) * 128, :], xh[:, k, :], reads=[("xh", k)])
                else:
                    P.dma("sp", "y%d" % k, ys[:, :], xh[:4, k, :], reads=[("xh", k)])

        try:
            for it in plan:
                if it[0] == "body":
                    body(it[1], it[2])
                elif it[0] == "body2":
                    body(it[1], 0, it[2])
                elif it[0] == "shared":
                    body(it[1], 0, it[2], phase="shared")
                elif it[0] == "rest":
                    body(it[1], 0, None, phase="rest", pk0=it[2])
                else:
                    ffn(it[1])
        except StopBuild:
            pass

        for h in (range(4) if stop >= 3 else []):
            P.dma("sp", "sfin", sp_o[h].rearrange("(x p) v -> p x v", p=128), S[:, 2 * h:2 * h + 2, :],
                  reads=[("S", h)])
        P.wait_all_dma("sp")

        with nc.Block() as block:
            P.finish(block)
    return nc


def _tables(hh):
    f32 = np.float32
    half = 128
    inv = (np.float32(10000.0) ** (-np.arange(half, dtype=f32) / f32(half))).astype(f32)
    cs = np.zeros((NCH, 128, 2, WS), f32)
    for c in range(NCH):
        base = c * 128 if hh == 1 else max(c - NPRE, 0) * 128
        pos = np.concatenate([base + np.arange(128), np.full(WS - 128, PAST)]).astype(f32)
        ang = (pos[:, None] * inv[None, :]).astype(f32)
        cs[c, :, 0, :] = np.cos(ang).T
        cs[c, :, 1, :] = np.sin(ang).T
    lg = np.log(1.0 - 2.0 ** (-5.0 - np.arange(4, dtype=np.float64)))
    i = np.arange(128, dtype=np.float64)
    retmask = np.zeros((4, 128, 128), np.float64)
    for h in range(4):
        m = np.exp(-lg[h] * (i[:, None] + 1.0)) / 16.0 * (i[:, None] <= i[None, :])
        retmask[h] = m
    cols = np.zeros((128, 16), np.float64)
    for h in range(4):
        cols[:, h] = np.exp(lg[h] * (127.0 - i)) / 16.0
        cols[:, 4 + h] = EPS * np.exp(-2.0 * lg[h] * (i + 1.0))
    cols[:, 8] = EPS
    for b in range(4):
        cols[b, 9 + b] = 1.0 / 16.0
    jj = np.arange(128)[:, None]
    ii = np.arange(128)[None, :]
    att = np.zeros((8, 128, 128), np.float64)
    k = 0
    for (win, dil) in GROUPS:
        res = ((ii - jj) % dil) == 0
        types = [res & (jj <= ii)]
        if dil > 1:
            types.append(res)
        types.append(res & (jj >= ii))
        for t in types:
            att[k] = np.where(t, 0.0, NEG)
            k += 1
    qmask = np.zeros((128, 16), np.float64)
    for b in range(4):
        qmask[:, 4 * b + b] = 1.0
    oh4 = np.eye(4)
    att_p = att if hh == 1 else np.full_like(att, NEG)
    return dict(cs_tab=cs, retmask=retmask.astype(f32), cols=cols.astype(f32), attmask=att.astype(f32),
                attmask_p=att_p.astype(f32), qmask=qmask.astype(f32), oh4=oh4.astype(f32))


_CACHE = {}


def kernel(x_prompt, x_sample, state_ret, cache_kv_w128, cache_kv_w512, cache_kv_w2048,
           ln1_g, w_in, ret_gn_g, w_pa, w_pb, w_o, ln2_g, w_ffn_in, w_ffn_out, lnf_g):
    f32 = np.float32
    if "nc" not in _CACHE:
        _CACHE["nc"] = build_program()
    nc = _CACHE["nc"]
    tabs = [_tables(0), _tables(1)]
    shared = dict(
        w_in=np.ascontiguousarray(w_in[0], f32), w_pa=np.ascontiguousarray(w_pa[0], f32),
        w_pb=np.ascontiguousarray(w_pb[0], f32), w_o=np.ascontiguousarray(w_o[0], f32),
        w_fi=np.ascontiguousarray(w_ffn_in[0], f32), w_fo=np.ascontiguousarray(w_ffn_out[0], f32),
        ln1=np.ascontiguousarray(ln1_g[0].reshape(8, 128).T, f32),
        ln2=np.ascontiguousarray(ln2_g[0].reshape(8, 128).T, f32),
        lnf=np.ascontiguousarray(np.broadcast_to(lnf_g[None, :], (128, D)), f32),
        gng=np.ascontiguousarray(ret_gn_g[0].reshape(16, 128).T, f32))
    cs_ = [cache_kv_w128, cache_kv_w512, cache_kv_w2048]
    in_maps = []
    for core in range(_CACHE.get("ncores", 8)):
        m = dict(shared)
        b, hh = core // 2, core % 2
        m.update(tabs[hh])
        if hh == 1:
            m["xp"] = np.ascontiguousarray(x_prompt[b], f32)
        else:
            m["xp"] = np.concatenate([np.zeros((NPRE * 128, D), f32), np.asarray(x_prompt[b, :SEQ - NPRE * 128], f32)], 0)
        m["xs"] = np.ascontiguousarray(x_sample[4 * core:4 * core + 4, 0, :], f32)
        m["st0"] = np.ascontiguousarray(state_ret[0, 4 * core:4 * core + 4], f32)
        for g in range(3):
            m["c%d" % g] = np.ascontiguousarray(cs_[g][0, 4 * core:4 * core + 4], f32)
        in_maps.append(m)
    res = run_bass_kernel_spmd(nc, in_maps, core_ids=list(range(len(in_maps))))
    r = res.results
    if len(r) < 8:
        return r
    y_prompt = np.stack([np.concatenate([r[2 * b]["yp"], r[2 * b + 1]["yp"]], 0) for b in range(4)], 0).astype(f32)
    y_sample = np.concatenate([r[k]["ys"] for k in range(8)], 0).reshape(32, 1, D).astype(f32)
    s_p = np.stack([r[2 * b + 1]["sp_o"] for b in range(4)], 0)[None].astype(f32)
    s_s = np.concatenate([r[k]["ss_o"] for k in range(8)], 0)[None].astype(f32)
    outs = [y_prompt, y_sample, s_p, s_s]
    for g in range(3):
        kvp = np.stack([r[2 * b + 1]["kvp%d" % g] for b in range(4)], 0)[None].astype(f32)
        kvs_ = np.concatenate([r[k]["kvs%d" % g] for k in range(8)], 0).reshape(1, 32, 1, 2, 4, 128).astype(f32)
        outs += [kvp, kvs_]
    return tuple(outs)
```
